# Optimizing a Trainium2 kernel written in Bass

```python
import math
import jax, jax.numpy as jnp
from jax import lax
import numpy as np

D_MODEL = 1024
BATCH = 2
SEQ = 8192
DEPTH = 1

D_MIX = D_MODEL
SB_HEADS = 8
SB_HEAD_DIM = 64
SB_WIDTH = SB_HEADS * SB_HEAD_DIM
DF_HEADS = 4
DF_HEAD_DIM = 64
DF_V_DIM = 2 * DF_HEAD_DIM
DF_QK_WIDTH = DF_HEADS * 2 * DF_HEAD_DIM
DF_WIDTH = DF_HEADS * DF_V_DIM
IN_COLS = 4 * SB_WIDTH + 2 * DF_QK_WIDTH + 2 * DF_WIDTH
BLOCK_Q = 128
EPS = 1e-6

kernel_name = "hybrid_stickbreak_diffattn_adaln_block"


def _rmsnorm(x, g):
    xf = x.astype(jnp.float32)
    y = xf * lax.rsqrt(jnp.mean(xf * xf, axis=-1, keepdims=True) + EPS)
    return (y * g.astype(jnp.float32)).astype(x.dtype)


def _alibi_slopes(n_heads):
    return jnp.asarray([2.0 ** (-8.0 * (h + 1) / n_heads) for h in range(n_heads)], dtype=jnp.float32)


def _to_blocks(t, nb):
    b, hh, s, d = t.shape
    return t.reshape(b, hh, nb, BLOCK_Q, d).transpose(2, 0, 1, 3, 4)


def _from_blocks(o):
    nb, b, hh, q, d = o.shape
    return o.transpose(1, 0, 3, 2, 4).reshape(b, nb * q, hh * d)


def _stick_breaking(q, k, v):
    s_len = k.shape[2]
    nb = s_len // BLOCK_Q
    inv = 1.0 / math.sqrt(q.shape[-1])
    kf = k.astype(jnp.float32)
    vf = v.astype(jnp.float32)
    spos = jnp.arange(s_len, dtype=jnp.int32)

    def block(args):
        qb, t0 = args
        z = jnp.einsum('bhqd,bhkd->bhqk', qb.astype(jnp.float32), kf) * inv
        tpos = t0 + jnp.arange(BLOCK_Q, dtype=jnp.int32)
        mask = spos[None, :] < tpos[:, None]
        log_1m = jnp.where(mask, jax.nn.log_sigmoid(-z), 0.0)
        rem = lax.cumsum(log_1m, axis=3, reverse=True) - log_1m
        a = jnp.where(mask, jnp.exp(jax.nn.log_sigmoid(z) + rem), 0.0)
        return jnp.einsum('bhqk,bhkd->bhqd', a, vf)

    starts = jnp.arange(nb, dtype=jnp.int32) * BLOCK_Q
    o = lax.map(block, (_to_blocks(q, nb), starts))
    return _from_blocks(o)


def _diff_attention(q1, q2, k1, k2, v, lam, slopes):
    s_len = k1.shape[2]
    nb = s_len // BLOCK_Q
    inv = 1.0 / math.sqrt(q1.shape[-1])
    k1f = k1.astype(jnp.float32)
    k2f = k2.astype(jnp.float32)
    vf = v.astype(jnp.float32)
    spos = jnp.arange(s_len, dtype=jnp.int32)

    def block(args):
        qb1, qb2, t0 = args
        tpos = t0 + jnp.arange(BLOCK_Q, dtype=jnp.int32)
        dist = (tpos[:, None] - spos[None, :]).astype(jnp.float32)
        mask = dist >= 0.0
        bias = -slopes[:, None, None] * dist
        s1 = jnp.einsum('bhqd,bhkd->bhqk', qb1.astype(jnp.float32), k1f) * inv + bias
        s2 = jnp.einsum('bhqd,bhkd->bhqk', qb2.astype(jnp.float32), k2f) * inv + bias
        p1 = jax.nn.softmax(jnp.where(mask, s1, -jnp.inf), axis=-1)
        p2 = jax.nn.softmax(jnp.where(mask, s2, -jnp.inf), axis=-1)
        return jnp.einsum('bhqk,bhkd->bhqd', p1 - lam * p2, vf)

    starts = jnp.arange(nb, dtype=jnp.int32) * BLOCK_Q
    o = lax.map(block, (_to_blocks(q1, nb), _to_blocks(q2, nb), starts))
    return o


def _layer(x, c, layer_idx, norm_g, w_ada, b_ada, w_in, q_norm_g, k_norm_g,
           lambda_q1, lambda_k1, lambda_q2, lambda_k2, subln_g, w_out):
    b, s, _ = x.shape
    mod = (c @ w_ada + b_ada).astype(jnp.float32)
    shift, scale, gate = jnp.split(mod, 3, axis=-1)
    h = (_rmsnorm(x, norm_g).astype(jnp.float32) * (1.0 + scale[:, None, :]) + shift[:, None, :]).astype(x.dtype)

    proj = h @ w_in
    cuts = np.cumsum([SB_WIDTH, SB_WIDTH, SB_WIDTH, SB_WIDTH, DF_QK_WIDTH, DF_QK_WIDTH, DF_WIDTH])
    sb_q, sb_k, sb_v, sb_g, df_q, df_k, df_v, df_g = jnp.split(proj, [int(i) for i in cuts], axis=-1)

    def heads(t, nh, d):
        return t.reshape(b, s, nh, d).transpose(0, 2, 1, 3)
    sb_out = _stick_breaking(heads(sb_q, SB_HEADS, SB_HEAD_DIM),
                             heads(sb_k, SB_HEADS, SB_HEAD_DIM),
                             heads(sb_v, SB_HEADS, SB_HEAD_DIM))
    sb_out = sb_out * jax.nn.silu(sb_g.astype(jnp.float32))

    qd = _rmsnorm(df_q.reshape(b, s, DF_HEADS, 2, DF_HEAD_DIM), q_norm_g)
    kd = _rmsnorm(df_k.reshape(b, s, DF_HEADS, 2, DF_HEAD_DIM), k_norm_g)
    q1 = qd[:, :, :, 0].transpose(0, 2, 1, 3)
    q2 = qd[:, :, :, 1].transpose(0, 2, 1, 3)
    k1 = kd[:, :, :, 0].transpose(0, 2, 1, 3)
    k2 = kd[:, :, :, 1].transpose(0, 2, 1, 3)
    vd = heads(df_v, DF_HEADS, DF_V_DIM)
    lam_init = 0.8 - 0.6 * math.exp(-0.3 * layer_idx)
    lam = (jnp.exp(jnp.sum(lambda_q1.astype(jnp.float32) * lambda_k1.astype(jnp.float32)))
           - jnp.exp(jnp.sum(lambda_q2.astype(jnp.float32) * lambda_k2.astype(jnp.float32)))
           + lam_init)
    df_o = _diff_attention(q1, q2, k1, k2, vd, lam, _alibi_slopes(DF_HEADS))
    df_o = _rmsnorm(df_o, subln_g) * (1.0 - lam_init)
    df_out = _from_blocks(df_o) * jax.nn.silu(df_g.astype(jnp.float32))

    mixed = jnp.concatenate([sb_out, df_out], axis=-1).astype(x.dtype)
    out = (mixed @ w_out).astype(jnp.float32)
    return (x.astype(jnp.float32) + gate[:, None, :] * out).astype(x.dtype)


def setup_inputs(seed: int = 0) -> dict:
    key = jax.random.key(seed)
    ks = jax.random.split(key, 16)
    f32 = jnp.float32
    d = D_MODEL
    return {
        "x": jax.random.normal(ks[0], (BATCH, SEQ, d), f32),
        "c": jax.random.normal(ks[1], (BATCH, d), f32),
        "norm_g": 1.0 + 0.02 * jax.random.normal(ks[2], (DEPTH, d), f32),
        "w_ada": 0.5 * d ** -0.5 * jax.random.normal(ks[3], (DEPTH, d, 3 * d), f32),
        "b_ada": 0.01 * jax.random.normal(ks[4], (DEPTH, 3 * d), f32),
        "w_in": d ** -0.5 * jax.random.normal(ks[5], (DEPTH, d, IN_COLS), f32),
        "q_norm_g": 1.0 + 0.02 * jax.random.normal(ks[6], (DEPTH, DF_HEAD_DIM), f32),
        "k_norm_g": 1.0 + 0.02 * jax.random.normal(ks[7], (DEPTH, DF_HEAD_DIM), f32),
        "lambda_q1": 0.1 * jax.random.normal(ks[8], (DEPTH, DF_HEAD_DIM), f32),
        "lambda_k1": 0.1 * jax.random.normal(ks[9], (DEPTH, DF_HEAD_DIM), f32),
        "lambda_q2": 0.1 * jax.random.normal(ks[10], (DEPTH, DF_HEAD_DIM), f32),
        "lambda_k2": 0.1 * jax.random.normal(ks[11], (DEPTH, DF_HEAD_DIM), f32),
        "subln_g": 1.0 + 0.02 * jax.random.normal(ks[12], (DEPTH, DF_V_DIM), f32),
        "w_out": D_MIX ** -0.5 * jax.random.normal(ks[13], (DEPTH, D_MIX, d), f32),
    }


def reference(x, c, norm_g, w_ada, b_ada, w_in, q_norm_g, k_norm_g,
              lambda_q1, lambda_k1, lambda_q2, lambda_k2, subln_g, w_out):
    for l in range(DEPTH):
        x = _layer(x, c, l, norm_g[l], w_ada[l], b_ada[l], w_in[l], q_norm_g[l], k_norm_g[l],
                   lambda_q1[l], lambda_k1[l], lambda_q2[l], lambda_k2[l], subln_g[l], w_out[l])
    return x
```

```python
import numpy as np
import ml_dtypes
import concourse.bass as bass
import concourse.mybir as mybir
from concourse.bass_utils import run_bass_kernel_spmd

F32 = mybir.dt.float32
BF16 = mybir.dt.bfloat16
AF = mybir.ActivationFunctionType
ALU = mybir.AluOpType
AX = mybir.AxisListType

S = 8192
D = 1024
NT = 16
NB = 64
EPS = 1e-6
LAM_INIT = 0.2
NEG = -30000.0
DBG = {}

ENGS = ("pe", "act", "dve", "pool", "sp")
NDSEM = 8


class Buf:
    __slots__ = ("w", "r", "name")

    def __init__(self, name=""):
        self.w = None
        self.r = {}
        self.name = name


class Prog:
    def __init__(self):
        self.q = {e: [] for e in ENGS}
        self.cnt = {e: 0 for e in ENGS}
        self.seen = {e: {} for e in ENGS}
        self.dma_n = {"sp": 0, "pool": 0, "act": 0}
        self.dma_last = {}
        self.cc_n = 0

    def _wait(self, eng, t, raw=False):
        key, val = t
        if key == eng:
            if eng == "pe" or (val < self.cnt[eng] and not raw):
                return
        if self.seen[eng].get(key, 0) >= val:
            return
        self.seen[eng][key] = val
        self.q[eng].append(("w", key, val))

    def _deps(self, eng, reads, writes, extra):
        deps = set(extra)
        rawd = set()
        for b in reads:
            if b.w is not None:
                rawd.add(b.w)
        for b in writes:
            if b.w is not None:
                deps.add(b.w)
            deps.update(b.r.values())
        for t in sorted(rawd):
            self._wait(eng, t, raw=True)
        for t in sorted(deps - rawd):
            self._wait(eng, t)

    def _mark(self, t, reads, writes):
        for b in reads:
            b.r[t[0]] = t
        for b in writes:
            b.w = t
            b.r = {}

    @staticmethod
    def _excl(reads, writes):
        r = [b for b in reads if not b.name.startswith("bank")]
        w = list(writes) + [b for b in reads if b.name.startswith("bank")]
        return r, w

    def op(self, eng, fn, reads=(), writes=(), extra=()):
        reads, writes = self._excl(reads, writes)
        self._deps(eng, reads, writes, extra)
        self.cnt[eng] += 1
        t = (eng, self.cnt[eng])
        self.q[eng].append(("i", fn))
        self._mark(t, reads, writes)
        return t

    def dma(self, queue, fn, reads=(), writes=(), extra=()):
        i = self.dma_n[queue]
        self.dma_n[queue] += 1
        slot = i % NDSEM
        val = 16 * (i // NDSEM + 1)
        key = "d_%s_%d" % (queue, slot)
        if i >= NDSEM:
            self._wait(queue, (key, val - 16))
        self._deps(queue, reads, writes, extra)
        self.q[queue].append(("d", fn, key))
        t = (key, val)
        self.dma_last[key] = t
        self._mark(t, reads, writes)
        return t

    def coll(self, fn, reads=(), writes=(), extra=()):
        self._deps("pool", reads, writes, extra)
        self.cc_n += 1
        self.q["pool"].append(("c", fn))
        t = ("cc", self.cc_n)
        self.dma_last["cc"] = t
        self._mark(t, reads, writes)
        return t

    def barrier(self, skip_cc=False):
        tickets = [(e, self.cnt[e]) for e in ENGS if self.cnt[e] > 0]
        tickets += [t for k, t in self.dma_last.items() if not (skip_cc and k == "cc")]
        for e in ENGS:
            for t in tickets:
                if t[0] != e:
                    self._wait(e, t)

    def sem_keys(self):
        keys = list(ENGS) + ["cc"]
        for qn in ("sp", "pool", "act"):
            for s in range(NDSEM):
                keys.append("d_%s_%d" % (qn, s))
        return keys

    def _signal_maps(self):
        waited = {e: set() for e in ENGS}
        for e in ENGS:
            for it in self.q[e]:
                if it[0] == "w" and it[1] in waited:
                    waited[it[1]].add(it[2])
        self.rank = {}
        for e in ENGS:
            self.rank[e] = {v: i + 1 for i, v in enumerate(sorted(waited[e]))}

    def replay(self, eng, e, sems):
        n = 0
        rank = self.rank
        for it in self.q[eng]:
            if it[0] == "w":
                if it[1] in rank:
                    e.wait_ge(sems[it[1]], rank[it[1]][it[2]])
                else:
                    e.wait_ge(sems[it[1]], it[2])
            elif it[0] == "i":
                n += 1
                ins = it[1](e)
                if n in rank[eng]:
                    ins.then_inc(sems[eng], 1)
            elif it[0] == "c":
                it[1](e).then_inc(sems["cc"], 1)
            else:
                it[1](e).then_inc(sems[it[2]], 16)

    def emit(self, nc):
        from contextlib import ExitStack
        self._signal_maps()
        with ExitStack() as st:
            sems = {}
            for k in self.sem_keys():
                sems[k] = st.enter_context(nc.semaphore(k))
            block = st.enter_context(nc.Block())

            @block.tensor
            def _(e):
                self.replay("pe", e, sems)

            @block.scalar
            def _(e):
                self.replay("act", e, sems)

            @block.vector
            def _(e):
                self.replay("dve", e, sems)

            @block.gpsimd
            def _(e):
                self.replay("pool", e, sems)

            @block.sync
            def _(e):
                self.replay("sp", e, sems)


CB_ID, CB_TRI, CB_ONES, CB_NSB, CB_NDF, CB_SEL = 0, 128, 256, 384, 512, 640
CB_W = 768
CF_BLK, CF_ALI = 0, 128
CF_W = 192


def _consts(hg):
    cb = np.zeros((128, CB_W), np.float32)
    i = np.arange(128)
    cb[:, CB_ID:CB_ID + 128] = np.eye(128)
    cb[:, CB_TRI:CB_TRI + 128] = np.where(i[:, None] >= i[None, :], -8.0, 0.0)
    cb[:, CB_ONES:CB_ONES + 128] = -8.0
    cb[:, CB_NSB:CB_NSB + 128] = np.where(i[:, None] < i[None, :], 0.0, NEG)
    cb[:, CB_NDF:CB_NDF + 128] = np.where(i[:, None] <= i[None, :], 0.0, NEG)
    cb[0, CB_SEL:CB_SEL + 128] = 1.0
    cb[32, CB_SEL:CB_SEL + 128] = 1.0
    cf = np.zeros((128, CF_W), np.float32)
    cf[:, CF_BLK:CF_BLK + 128] = (i[:, None] // 64 == i[None, :] // 64)
    slope = 2.0 ** (-2.0 * (hg + 1))
    m = np.arange(64)
    cf[:, CF_ALI:CF_ALI + 64] = slope * (i[:, None] + 129.0 - 128.0 * m[None, :])
    return cb.astype(ml_dtypes.bfloat16), cf


def build(phase=9, debug=False):
    nc = bass.Bass("TRN2", target_bir_lowering=False)
    P = Prog()

    def din(name, shape, dt=F32):
        return nc.dram_tensor(name, list(shape), dt, kind="ExternalInput").ap()

    x_d = din("x", [S, D])
    xq_d = din("xq", [S, 256])
    ccol_d = din("ccol", [128, 8])
    ngcol_d = din("ngcol", [128, 8])
    bcol_d = din("bcol", [128, 16])
    wss_d = din("wss", [D, 2048])
    wg_d = din("wg", [D, 256])
    bg_d = din("bg", [256])
    win_d = din("win", [D, 1024])
    qkg_d = din("qkg", [128, 2])
    lamv_d = din("lamv", [256])
    subg_d = din("subg", [128])
    wout_d = din("wout", [D, 256])
    cb_d = din("cb", [128, CB_W], BF16)
    cf_d = din("cf", [128, CF_W])
    out_d = nc.dram_tensor("out", [S, 256], F32, kind="ExternalOutput").ap()

    gsb_d = nc.dram_tensor("gsb_s", [S, 128], F32).ap()
    gdf_d = nc.dram_tensor("gdf_s", [S, 128], F32).ap()
    CH = [(0, 4), (4, 8), (8, 12), (12, 15), (15, 16)]
    NCH = len(CH)
    ch_of = {qi: c for c, (a, b_) in enumerate(CH) for qi in range(a, b_)}
    ch_end = {b_ - 1: c for c, (a, b_) in enumerate(CH)}
    mixT_q = [nc.dram_tensor("mixT_s%d" % c, [256, (b_ - a) * 512], BF16).ap() for c, (a, b_) in enumerate(CH)]
    gath_q = [nc.dram_tensor("gath_s%d" % c, [1024, (b_ - a) * 512], BF16).ap() for c, (a, b_) in enumerate(CH)]
    mix_tickets = [[] for _ in range(NCH)]

    def sb(name, shape, dt):
        return nc.alloc_sbuf_tensor("sb_" + name, list(shape), dt).ap()

    QsbT = sb("QsbT", [128, S], BF16)
    KsbT = sb("KsbT", [128, S], BF16)
    QdfT = sb("QdfT", [128, S], BF16)
    KdfT = sb("KdfT", [128, S], BF16)
    Vsb = sb("Vsb", [128, NB * 128], BF16)
    Vdf = sb("Vdf", [128, NB * 130], BF16)
    cb = sb("cb", [128, CB_W], BF16)
    cf = sb("cf", [128, CF_W], F32)
    small = sb("small", [128, 1024], F32)
    scr = sb("scr", [128, 18432], F32)
    ps = nc.alloc_psum_tensor("ps", [128, 8, 512], F32).ap()

    ident = cb[:, CB_ID:CB_ID + 128]
    tri = cb[:, CB_TRI:CB_TRI + 128]
    onesm8 = cb[:, CB_ONES:CB_ONES + 128]
    negsb = cb[:, CB_NSB:CB_NSB + 128]
    negdf = cb[:, CB_NDF:CB_NDF + 128]
    sel = cb[:, CB_SEL:CB_SEL + 128]
    blockones = cf[:, CF_BLK:CF_BLK + 128]

    ccol = small[:, 0:8]
    ngcol = small[:, 8:16]
    bcol = small[:, 16:32]
    qkg = small[:, 32:34]
    g8 = small[:, 34:36]
    shiftT = small[:, 36:44]
    Acol = small[:, 44:52]
    biasT = small[:, 52:56]
    lam_t = small[:, 56:60]
    ccol2 = small[:, 64:80]
    shift2 = small[:, 80:96]
    ss_x = small[:, 96:160]
    r_x = small[:, 160:224]
    gate_rep = small[:, 256:512]
    subg8 = small[:, 512:640]
    lamv = small[:, 640:896]
    tmp_s = small[:, 896:1024]

    B = {}

    def buf(n):
        if n not in B:
            B[n] = Buf(n)
        return B[n]

    bank = [buf("bank%d" % i) for i in range(8)]

    P.dma("sp", lambda e: e.dma_start(out=cb, in_=cb_d), writes=[buf("cb")])
    P.dma("sp", lambda e: e.dma_start(out=cf, in_=cf_d), writes=[buf("cf")])
    P.dma("sp", lambda e: e.dma_start(out=ccol, in_=ccol_d), writes=[buf("vec")])
    P.dma("sp", lambda e: e.dma_start(out=ngcol, in_=ngcol_d), writes=[buf("vec")])
    P.dma("sp", lambda e: e.dma_start(out=bcol, in_=bcol_d), writes=[buf("vec")])
    P.dma("sp", lambda e: e.dma_start(out=qkg, in_=qkg_d), writes=[buf("vec")])
    P.dma("sp", lambda e: e.dma_start(out=lamv, in_=lamv_d.partition_broadcast(128)), writes=[buf("vec")])
    P.dma("sp", lambda e: e.dma_start(out=subg8, in_=subg_d.partition_broadcast(128)), writes=[buf("vec")])
    P.dma("sp", lambda e: e.dma_start(out=gate_rep, in_=bg_d.partition_broadcast(128)), writes=[buf("gate")])

    Vdf3 = Vdf.rearrange("p (k c) -> p k c", c=130)
    P.op("pool", lambda e: e.memset(Vdf3[:, :, 128:130], 1.0), writes=[buf("Vdf")])

    cc2 = ccol2.rearrange("p (k c) -> p k c", c=2)
    for j in range(2):
        P.op("dve", lambda e, j=j: e.tensor_copy(out=cc2[:, :, j], in_=ccol), reads=[buf("vec")], writes=[buf("cc2")])
    P.op("dve", lambda e: e.tensor_scalar(out=subg8, in0=subg8, scalar1=1.0 - LAM_INIT, scalar2=None, op0=ALU.mult),
         reads=[buf("vec")], writes=[buf("subg8")])
    P.op("dve", lambda e: e.tensor_scalar(out=g8, in0=qkg, scalar1=8.0, scalar2=None, op0=ALU.mult),
         reads=[buf("vec")], writes=[buf("g8")])
    P.op("dve", lambda e: e.tensor_tensor(out=tmp_s[:, 0:64], in0=lamv[:, 0:64], in1=lamv[:, 64:128], op=ALU.mult),
         reads=[buf("vec")], writes=[buf("tmp_s")])
    P.op("dve", lambda e: e.tensor_tensor(out=tmp_s[:, 64:128], in0=lamv[:, 128:192], in1=lamv[:, 192:256], op=ALU.mult),
         reads=[buf("vec")], writes=[buf("tmp_s")])
    P.op("dve", lambda e: e.reduce_sum(out=lam_t[:, 0:2], in_=tmp_s.rearrange("p (a b) -> p a b", a=2), axis=AX.X),
         reads=[buf("tmp_s")], writes=[buf("lam")])
    P.op("act", lambda e: e.activation(out=lam_t[:, 0:2], in_=lam_t[:, 0:2], func=AF.Exp),
         reads=[buf("lam")], writes=[buf("lam")])
    P.op("dve", lambda e: e.tensor_tensor(out=lam_t[:, 2:3], in0=lam_t[:, 0:1], in1=lam_t[:, 1:2], op=ALU.subtract),
         reads=[buf("lam")], writes=[buf("lam2")])
    P.op("dve", lambda e: e.tensor_scalar(out=lam_t[:, 3:4], in0=lam_t[:, 2:3], scalar1=LAM_INIT, scalar2=-1.0,
                                          op0=ALU.add, op1=ALU.mult),
         reads=[buf("lam2")], writes=[buf("nlam")])
    nlam = lam_t[:, 3:4]

    wst = [scr[:, i * 4096:(i + 1) * 4096] for i in range(2)]
    wss_v = wss_d.rearrange("(k p) n -> p k n", p=128)
    modps = ps[:, 7, 0:32].rearrange("p (n c) -> p n c", c=2)
    first = True
    for pc in range(4):
        sl = pc % 2
        w3 = wst[sl].rearrange("p (k n) -> p k n", k=8)
        P.dma("sp", lambda e, w3=w3, pc=pc: e.dma_start(out=w3, in_=wss_v[:, :, pc * 512:(pc + 1) * 512]),
              writes=[buf("wst%d" % sl)])
        for nb in range(4):
            for kc in range(8):
                P.op("pe", lambda e, w3=w3, nb=nb, kc=kc, pc=pc, st=first: e.matmul(
                    modps[:, pc * 4 + nb, :], lhsT=w3[:, kc, nb * 128:(nb + 1) * 128], rhs=cc2[:, kc, :],
                    start=st, stop=(kc == 7), skip_group_check=True),
                    reads=[buf("wst%d" % sl), buf("cc2")], writes=[bank[7]])
                first = False
    P.op("dve", lambda e: e.tensor_tensor(out=shiftT, in0=modps[:, 0:8, 0], in1=bcol[:, 0:8], op=ALU.add),
         reads=[bank[7], buf("vec")], writes=[buf("shiftT")])
    P.op("dve", lambda e: e.tensor_tensor(out=Acol, in0=modps[:, 8:16, 0], in1=bcol[:, 8:16], op=ALU.add),
         reads=[bank[7], buf("vec")], writes=[buf("Acol")])
    P.op("dve", lambda e: e.tensor_scalar(out=Acol, in0=Acol, scalar1=1.0, scalar2=32.0, op0=ALU.add, op1=ALU.mult),
         reads=[buf("Acol")], writes=[buf("Acol")])
    P.op("pool", lambda e: e.tensor_tensor(out=Acol, in0=Acol, in1=ngcol, op=ALU.mult),
         reads=[buf("Acol"), buf("vec")], writes=[buf("Acol")])
    sh2 = shift2.rearrange("p (k c) -> p k c", c=2)
    for j in range(2):
        P.op("dve", lambda e, j=j: e.tensor_copy(out=sh2[:, :, j], in_=shiftT), reads=[buf("shiftT")], writes=[buf("sh2")])

    crep = scr[:, 8192:9216].rearrange("p (k m) -> p k m", k=8)
    wgst = scr[:, 9216:11264].rearrange("p (k n) -> p k n", k=8)
    P.dma("sp", lambda e: e.dma_start(out=wgst, in_=wg_d.rearrange("(k p) n -> p k n", p=128)), writes=[buf("wgst")])
    for kc in range(8):
        P.op("pool", lambda e, kc=kc: e.tensor_copy(out=crep[:, kc, :], in_=ccol[:, kc:kc + 1].to_broadcast([128, 128])),
             reads=[buf("vec")], writes=[buf("crep")])
    for kc in range(8):
        P.op("pe", lambda e, kc=kc: e.matmul(ps[:, 6, 0:256], lhsT=crep[:, kc, :], rhs=wgst[:, kc, :],
                                             start=(kc == 0), stop=(kc == 7)),
             reads=[buf("crep"), buf("wgst")], writes=[bank[6]])
    P.op("dve", lambda e: e.tensor_tensor(out=gate_rep, in0=ps[:, 6, 0:256], in1=gate_rep, op=ALU.add),
         reads=[bank[6], buf("gate")], writes=[buf("gate")])

    Wp = scr[:, 11264:15360].bitcast(BF16).rearrange("p (k n) -> p k n", k=8)
    wst2 = [scr[:, i * 1024:(i + 1) * 1024] for i in range(2)]
    win_v = win_d.rearrange("(k p) n -> p k n", p=128)
    bTps = ps[:, 5, 0:8].rearrange("p (g c) -> p g c", c=2)
    browps = ps[:, 4, :]
    shrep = scr[:, 15360:16384].rearrange("p (k m) -> p k m", k=8)
    for kc in range(8):
        P.op("pool", lambda e, kc=kc: e.tensor_copy(out=shrep[:, kc, :], in_=shiftT[:, kc:kc + 1].to_broadcast([128, 128])),
             reads=[buf("shiftT")], writes=[buf("shrep")])
    for kc in range(8):
        sl = kc % 2
        P.dma("sp", lambda e, kc=kc, sl=sl: e.dma_start(out=wst2[sl], in_=win_v[:, kc, :]),
              writes=[buf("w2st%d" % sl)] + ([buf("wst0"), buf("wst1")] if kc < 2 else []))
        for g in range(4):
            P.op("pe", lambda e, kc=kc, sl=sl, g=g: e.matmul(
                bTps[:, g, :], lhsT=wst2[sl][:, g * 128:(g + 1) * 128], rhs=sh2[:, kc, :],
                start=(kc == 0 and g == 0), stop=(kc == 7), skip_group_check=True),
                reads=[buf("w2st%d" % sl), buf("sh2")], writes=[bank[5]])
        P.op("pe", lambda e, kc=kc, sl=sl: e.matmul(
            browps, lhsT=shrep[:, kc, :], rhs=wst2[sl][:, 512:1024], start=(kc == 0), stop=(kc == 7)),
            reads=[buf("w2st%d" % sl), buf("shrep")], writes=[bank[4]])
        P.op("act", lambda e, kc=kc, sl=sl: e.activation(out=Wp[:, kc, :], in_=wst2[sl], func=AF.Copy,
                                                         scale=Acol[:, kc:kc + 1]),
             reads=[buf("w2st%d" % sl), buf("Acol")], writes=[buf("Wp")])
    P.op("dve", lambda e: e.tensor_copy(out=biasT, in_=bTps[:, :, 0]), reads=[bank[5]], writes=[buf("biasT")])
    HL = scr[:, 16384:16640].bitcast(BF16)
    hi_rep = scr[:, 16640:16896].bitcast(BF16)
    brow = scr[:, 16896:17408]
    P.op("dve", lambda e: e.tensor_copy(out=brow, in_=browps), reads=[bank[4]], writes=[buf("brow")])

    dbg = {}
    if phase == 0:
        if debug:
            P.barrier()
            d = nc.dram_tensor("dbg_small", [128, 1024], F32, kind="ExternalOutput").ap()
            P.dma("sp", lambda e, d=d: e.dma_start(out=d, in_=small))
            d = nc.dram_tensor("dbg_scr", [128, 18432], F32, kind="ExternalOutput").ap()
            P.dma("sp", lambda e, d=d: e.dma_start(out=d, in_=scr))
        return _finish(nc, P, B, dbg, locals())

    xt = [scr[:, i * 1024:(i + 1) * 1024] for i in range(3)]
    xn = [scr[:, 3072 + i * 512:3072 + (i + 1) * 512].bitcast(BF16) for i in range(2)]
    junk = scr[:, 4096:4608].bitcast(BF16)
    hT = [scr[:, 4608 + i * 2048:4608 + (i + 1) * 2048].bitcast(BF16).rearrange("p (k t) -> p k t", k=8)
          for i in range(2)]
    qb = [scr[:, 8704:9216], scr[:, 17408:17920]]
    sq = [scr[:, 9216 + i * 512:9216 + (i + 1) * 512] for i in range(2)]
    rr = [scr[:, 10240 + i * 512:10240 + (i + 1) * 512] for i in range(2)]
    ge = [scr[:, 15360 + i * 256:15360 + (i + 1) * 256] for i in range(2)]
    gs = [scr[:, 15872 + i * 256:15872 + (i + 1) * 256] for i in range(2)] + \
         [scr[:, 17920 + i * 256:17920 + (i + 1) * 256] for i in range(2)]
    tps = [ps[:, i, :].bitcast(BF16) for i in range(2)]
    pstm = [ps[:, 2 + i, :] for i in range(2)]
    psfm = [ps[:, 4 + i, :] for i in range(2)]
    psss = ps[:, 6, :]
    P.barrier()

    QT = {0: QsbT, 1: KsbT, 2: QdfT, 3: KdfT}
    NBLK = DBG.get('tiles', NT) * 4

    def stA(bi):
        xs, ns = bi % 3, bi % 2
        P.dma("sp", lambda e, bi=bi, xs=xs: e.dma_start(out=xt[xs], in_=x_d[bi * 128:(bi + 1) * 128, :]),
              writes=[buf("xt%d" % xs)])
        P.op("act", lambda e, bi=bi, xs=xs: e.activation(out=junk, in_=xt[xs], func=AF.Square,
                                                         accum_out=ss_x[:, bi:bi + 1]),
             reads=[buf("xt%d" % xs)], writes=[buf("junk"), buf("ssx%d" % ns)])

    def stA2(bi):
        xs, ns = bi % 3, bi % 2
        P.op("act", lambda e, bi=bi: e.activation(out=r_x[:, bi:bi + 1], in_=ss_x[:, bi:bi + 1], func=AF.Ln,
                                                  bias=float(D * EPS)),
             reads=[buf("ssx%d" % ns)], writes=[buf("rx%d" % ns)])

    def stA3(bi):
        xs, ns = bi % 3, bi % 2
        P.op("act", lambda e, bi=bi: e.activation(out=r_x[:, bi:bi + 1], in_=r_x[:, bi:bi + 1], func=AF.Exp,
                                                  scale=-0.5),
             reads=[buf("rx%d" % ns)], writes=[buf("rx%d" % ns)])
        P.op("dve", lambda e, bi=bi, xs=xs, ns=ns: e.tensor_scalar(out=xn[ns], in0=xt[xs], scalar1=r_x[:, bi:bi + 1],
                                                                   scalar2=None, op0=ALU.mult),
             reads=[buf("xt%d" % xs), buf("rx%d" % ns)], writes=[buf("xn%d" % ns)])

    def stB(bi):
        ns, hs, sub = bi % 2, (bi // 4) % 2, bi % 4
        for kc in range(8):
            P.op("pe", lambda e, kc=kc, ns=ns: e.transpose(tps[ns][:, kc * 128:(kc + 1) * 128],
                                                           xn[ns][:, kc * 128:(kc + 1) * 128], ident),
                 reads=[buf("xn%d" % ns), buf("cb")], writes=[bank[ns]])
        P.op("dve", lambda e, ns=ns, hs=hs, sub=sub: e.tensor_copy(
            out=hT[hs][:, :, sub * 128:(sub + 1) * 128], in_=tps[ns].rearrange("p (k t) -> p k t", k=8)),
            reads=[bank[ns]], writes=[buf("hT%d" % hs)])

    def stC1(bi):
        hs, sub, tm = (bi // 4) % 2, bi % 4, bi % 2
        for kc in range(8):
            P.op("pe", lambda e, kc=kc, hs=hs, sub=sub, tm=tm: e.matmul(
                pstm[tm], lhsT=hT[hs][:, kc, sub * 128:(sub + 1) * 128], rhs=Wp[:, kc, 512:1024],
                start=(kc == 0), stop=(kc == 7)),
                reads=[buf("hT%d" % hs), buf("Wp")], writes=[bank[2 + tm]])

    def stC2(bi):
        tm = bi % 2
        gin = pstm[tm].rearrange("p (a two b) -> p a two b", two=2, b=128)[:, :, 1, :]
        bg_ = brow.rearrange("p (a two b) -> p a two b", two=2, b=128)[:, :, 1, :]
        ge3 = ge[tm].rearrange("p (a b) -> p a b", a=2)
        g4 = bi % 4
        gs3 = gs[g4].rearrange("p (a b) -> p a b", a=2)
        gc3 = scr[:, 16384 + tm * 256:16384 + (tm + 1) * 256].rearrange("p (a b) -> p a b", a=2)
        P.op("dve", lambda e, gin=gin, gc3=gc3, bg_=bg_: e.tensor_tensor(out=gc3, in0=gin, in1=bg_, op=ALU.add),
             reads=[bank[2 + tm], buf("brow")], writes=[buf("gc%d" % tm)])
        P.op("dve", lambda e, tm=tm, bi=bi: e.tensor_tensor(out=Vsb[:, bi * 128:(bi + 1) * 128], in0=pstm[tm][:, 0:128],
                                                            in1=brow[:, 0:128], op=ALU.add),
             reads=[bank[2 + tm], buf("brow")], writes=[buf("Vsb")])
        P.op("dve", lambda e, tm=tm, bi=bi: e.tensor_tensor(out=Vdf3[:, bi, 0:128], in0=pstm[tm][:, 256:384],
                                                            in1=brow[:, 256:384], op=ALU.add),
             reads=[bank[2 + tm], buf("brow")], writes=[buf("Vdf")])
        P.op("act", lambda e, gc3=gc3, ge3=ge3: e.activation(out=ge3, in_=gc3, func=AF.Exp, scale=-1.0),
             reads=[buf("gc%d" % tm)], writes=[buf("ge%d" % tm)])
        P.op("act", lambda e, ge3=ge3: e.activation(out=ge3, in_=ge3, func=AF.Ln, bias=1.0),
             reads=[buf("ge%d" % tm)], writes=[buf("ge%d" % tm)])
        P.op("act", lambda e, ge3=ge3: e.activation(out=ge3, in_=ge3, func=AF.Exp, scale=-1.0),
             reads=[buf("ge%d" % tm)], writes=[buf("ge%d" % tm)])
        P.op("dve", lambda e, gc3=gc3, ge3=ge3, gs3=gs3: e.tensor_tensor(out=gs3, in0=gc3, in1=ge3, op=ALU.mult),
             reads=[buf("gc%d" % tm), buf("ge%d" % tm)], writes=[buf("gs%d" % g4)])
        P.op("pool", lambda e, gs3=gs3: e.tensor_tensor(out=gs3[:, 1, :], in0=gs3[:, 1, :], in1=subg8, op=ALU.mult),
             reads=[buf("gs%d" % g4), buf("subg8")], writes=[buf("gs%d" % g4)])
        P.dma("pool", lambda e, gs3=gs3, bi=bi: e.dma_start(out=gsb_d[bi * 128:(bi + 1) * 128, :], in_=gs3[:, 0, :]),
              reads=[buf("gs%d" % g4)])
        P.dma("pool", lambda e, gs3=gs3, bi=bi: e.dma_start(out=gdf_d[bi * 128:(bi + 1) * 128, :], in_=gs3[:, 1, :]),
              reads=[buf("gs%d" % g4)])

    def stD_mm(ti, g):
        hs, fs = ti % 2, g % 2
        for kc in range(8):
            P.op("pe", lambda e, kc=kc, g=g, hs=hs, fs=fs: e.matmul(
                psfm[fs], lhsT=Wp[:, kc, g * 128:(g + 1) * 128], rhs=hT[hs][:, kc, :],
                start=(kc == 0), stop=(kc == 7)),
                reads=[buf("hT%d" % hs), buf("Wp")], writes=[bank[4 + fs]])

    def stD_ev(ti, g):
        fs = g % 2
        dst = QT[g][:, ti * 512:(ti + 1) * 512]
        if g < 2:
            P.op("act", lambda e, fs=fs, dst=dst, g=g: e.activation(out=dst, in_=psfm[fs], func=AF.Identity,
                                                                    bias=biasT[:, g:g + 1]),
                 reads=[bank[4 + fs], buf("biasT")], writes=[buf("QT%d" % g)])
        else:
            qs = g % 2
            P.op("act", lambda e, fs=fs, qs=qs, g=g: e.activation(out=qb[qs], in_=psfm[fs], func=AF.Identity,
                                                                  bias=biasT[:, g:g + 1]),
                 reads=[bank[4 + fs], buf("biasT")], writes=[buf("qb%d" % qs)])
            P.op("act", lambda e, fs=fs, qs=qs, g=g: e.activation(out=sq[qs], in_=psfm[fs], func=AF.Square,
                                                                  bias=biasT[:, g:g + 1]),
                 reads=[bank[4 + fs], buf("biasT")], writes=[buf("sq%d" % qs)])
            P.op("pe", lambda e, qs=qs: e.matmul(ps[:, 6 + qs, :], lhsT=blockones, rhs=sq[qs], start=True, stop=True),
                 reads=[buf("sq%d" % qs), buf("cf")], writes=[bank[6 + qs]])

    def stD_fin(ti, g):
        qs = g % 2
        dst = QT[g][:, ti * 512:(ti + 1) * 512]
        P.op("act", lambda e, qs=qs: e.activation(out=rr[qs], in_=ps[:, 6 + qs, :], func=AF.Ln, bias=float(64 * EPS)),
             reads=[bank[6 + qs]], writes=[buf("rr%d" % qs)])
        P.op("act", lambda e, qs=qs: e.activation(out=rr[qs], in_=rr[qs], func=AF.Exp, scale=-0.5),
             reads=[buf("rr%d" % qs)], writes=[buf("rr%d" % qs)])
        P.op("dve", lambda e, qs=qs, dst=dst, g=g: e.scalar_tensor_tensor(
            out=dst, in0=qb[qs], scalar=g8[:, g - 2:g - 1], in1=rr[qs], op0=ALU.mult, op1=ALU.mult),
            reads=[buf("qb%d" % qs), buf("rr%d" % qs), buf("g8")], writes=[buf("QT%d" % g)])

    def stB1(bi):
        ns = bi % 2
        for kc in range(8):
            P.op("pe", lambda e, kc=kc, ns=ns: e.transpose(tps[ns][:, kc * 128:(kc + 1) * 128],
                                                           xn[ns][:, kc * 128:(kc + 1) * 128], ident),
                 reads=[buf("xn%d" % ns), buf("cb")], writes=[bank[ns]])

    def stB2(bi):
        ns, hs, sub = bi % 2, (bi // 4) % 2, bi % 4
        P.op("dve", lambda e, ns=ns, hs=hs, sub=sub: e.tensor_copy(
            out=hT[hs][:, :, sub * 128:(sub + 1) * 128], in_=tps[ns].rearrange("p (k t) -> p k t", k=8)),
            reads=[bank[ns]], writes=[buf("hT%d" % hs)])

    dq_mm, dq_ev = [], []
    for k in range(NBLK + 12):
        if k < NBLK:
            stA(k)
        if 1 <= k <= NBLK:
            stA2(k - 1)
        if 4 <= k <= NBLK + 3:
            stC1(k - 4)
        if 1 <= k <= NBLK:
            stA3(k - 1)
        if 2 <= k <= NBLK + 1:
            stB1(k - 2)
        if 3 <= k <= NBLK + 2:
            stB2(k - 3)
            if (k - 3) % 4 == 3:
                ti = (k - 3) // 4
                for g in range(4):
                    dq_mm.append((ti, g))
        if 5 <= k <= NBLK + 4:
            stC2(k - 5)
        nev = len(dq_ev)
        for _ in range(nev):
            kind, ti, g = dq_ev.pop(0)
            {"ev": stD_ev, "fin": stD_fin}[kind](ti, g)
            if kind == "ev" and g >= 2:
                dq_ev.append(("fin", ti, g))
        for _ in range(2):
            if dq_mm:
                ti, g = dq_mm.pop(0)
                stD_mm(ti, g)
                dq_ev.append(("ev", ti, g))
    while dq_mm or dq_ev:
        nev = len(dq_ev)
        for _ in range(nev):
            kind, ti, g = dq_ev.pop(0)
            {"ev": stD_ev, "fin": stD_fin}[kind](ti, g)
            if kind == "ev" and g >= 2:
                dq_ev.append(("fin", ti, g))
        if dq_mm:
            ti, g = dq_mm.pop(0)
            stD_mm(ti, g)
            dq_ev.append(("ev", ti, g))

    if phase == 2:
        if debug:
            for nm, t, w in (("QsbT", QsbT, S), ("KsbT", KsbT, S), ("QdfT", QdfT, S), ("KdfT", KdfT, S),
                             ("Vsb", Vsb, NB * 128), ("Vdf", Vdf, NB * 130)):
                d = nc.dram_tensor("dbg_" + nm, [128, w], BF16, kind="ExternalOutput").ap()
                dbg[nm] = d
                P.barrier()
                P.dma("sp", lambda e, d=d, t=t: e.dma_start(out=d, in_=t))
            d = nc.dram_tensor("dbg_small", [128, 1024], F32, kind="ExternalOutput").ap()
            P.dma("sp", lambda e, d=d: e.dma_start(out=d, in_=small))
            d2 = nc.dram_tensor("dbg_gsb", [S, 128], F32, kind="ExternalOutput").ap()
            P.dma("sp", lambda e, d2=d2: e.dma_start(out=d2, in_=gsb_d))
            d3 = nc.dram_tensor("dbg_gdf", [S, 128], F32, kind="ExternalOutput").ap()
            P.dma("sp", lambda e, d3=d3: e.dma_start(out=d3, in_=gdf_d))
        return _finish(nc, P, B, dbg, locals())

    P.barrier()
    e_t = [scr[:, i * 1024:(i + 1) * 1024].rearrange("p (h t) -> p h t", h=2) for i in range(2)]
    spb = [scr[:, 2048 + i * 512:2048 + (i + 1) * 512].bitcast(BF16).rearrange("p (h t) -> p h t", h=2) for i in range(2)]
    Lsum = scr[:, 3072:4096].rearrange("p (h t) -> p h t", h=2)
    LsB = [scr[:, 4096 + i * 512:4096 + (i + 1) * 512].bitcast(BF16).rearrange("p (h t) -> p h t", h=2) for i in range(2)]
    ATt = [scr[:, 5120 + i * 512:5120 + (i + 1) * 512].bitcast(BF16).rearrange("p (h t) -> p h t", h=2) for i in range(3)]
    gt = [scr[:, 6656 + i * 512:6656 + (i + 1) * 512].rearrange("p (s f) -> p s f", s=4) for i in range(2)]
    mix = [scr[:, 7680 + i * 256:7680 + (i + 1) * 256].bitcast(BF16).rearrange("p (s f) -> p s f", s=4) for i in range(2)]
    mixT = [scr[:, 8192 + i * 256:8192 + (i + 1) * 256].bitcast(BF16) for i in range(2)]
    tpb = ps[:, 6, :].bitcast(BF16)

    def sb_steps():
        out = []
        for qi in range(DBG.get('sb_tiles', NT)):
            for m in range(4 * qi + 4):
                kb = 4 * qi + 3 - m
                j = kb - 4 * qi
                out.append(dict(qi=qi, m=m, kb=kb, j=j, c0=128 * max(j, 0), last=(kb == 0)))
        for gi, st in enumerate(out):
            st["gi"] = gi
        return out

    steps = sb_steps()
    nst = len(steps)

    def QK(st):
        sl = st["gi"] % 2
        c0, kb, qi = st["c0"], st["kb"], st["qi"]
        for h in range(2):
            P.op("pe", lambda e, h=h, sl=sl, c0=c0, kb=kb, qi=qi: e.matmul(
                ps[:, 2 * sl + h, c0:512], lhsT=KsbT[64 * h:64 * h + 64, kb * 128:(kb + 1) * 128],
                rhs=QsbT[64 * h:64 * h + 64, qi * 512 + c0:(qi + 1) * 512], start=True, stop=False, skip_group_check=True),
                reads=[buf("QT0"), buf("QT1")], writes=[bank[2 * sl + h]])
        if st["j"] >= 0:
            for h in range(2):
                P.op("pe", lambda e, h=h, sl=sl, c0=c0: e.matmul(
                    ps[:, 2 * sl + h, c0:c0 + 128], lhsT=ident, rhs=negsb, start=False, stop=False, skip_group_check=True),
                    reads=[buf("cb")], writes=[bank[2 * sl + h]])

    def EXP1(st):
        sl = st["gi"] % 2
        c0 = st["c0"]
        P.op("act", lambda e, sl=sl, c0=c0: e.activation(out=e_t[sl][:, :, c0:512], in_=ps[:, 2 * sl:2 * sl + 2, c0:512],
                                                         func=AF.Exp, scale=0.125),
             reads=[bank[2 * sl], bank[2 * sl + 1]], writes=[buf("e%d" % sl)])

    def LN(st):
        sl = st["gi"] % 2
        c0 = st["c0"]
        P.op("act", lambda e, sl=sl, c0=c0: e.activation(out=spb[sl][:, :, c0:512], in_=e_t[sl][:, :, c0:512],
                                                         func=AF.Ln, bias=1.0),
             reads=[buf("e%d" % sl)], writes=[buf("sp%d" % sl)])

    def LSUM(st):
        gi, sl, c0 = st["gi"], st["gi"] % 2, st["c0"]
        if st["m"] == 0:
            P.op("pool", lambda e: e.memset(Lsum, 0.0), writes=[buf("Lsum")])
        P.op("dve", lambda e, sl=sl, c0=c0: e.tensor_tensor(out=Lsum[:, :, c0:512], in0=Lsum[:, :, c0:512],
                                                            in1=spb[sl][:, :, c0:512], op=ALU.add),
             reads=[buf("sp%d" % sl)], writes=[buf("Lsum")])
        if not st["last"]:
            nxt = steps[gi + 1]
            lb = (gi + 1) % 2
            cn = nxt["c0"]
            P.op("dve", lambda e, lb=lb, cn=cn: e.tensor_copy(out=LsB[lb][:, :, cn:512], in_=Lsum[:, :, cn:512]),
                 reads=[buf("Lsum")], writes=[buf("LsB%d" % lb)])

    def TRI(st):
        gi, sl, c0 = st["gi"], st["gi"] % 2, st["c0"]
        for h in range(2):
            P.op("pe", lambda e, h=h, sl=sl, c0=c0: e.matmul(
                ps[:, 2 * sl + h, c0:512], lhsT=tri, rhs=spb[sl][:, h, c0:512], start=False, stop=True,
                skip_group_check=True),
                reads=[buf("sp%d" % sl), buf("cb")], writes=[bank[2 * sl + h]])

    def CARRY(st):
        gi, sl, c0 = st["gi"], st["gi"] % 2, st["c0"]
        if st["m"] == 0:
            return
        lb = gi % 2
        for h in range(2):
            P.op("pe", lambda e, h=h, sl=sl, c0=c0, lb=lb: e.matmul(
                ps[:, 2 * sl + h, c0:512], lhsT=onesm8, rhs=LsB[lb][:, h, c0:512], start=False, stop=False,
                skip_group_check=True),
                reads=[buf("LsB%d" % lb), buf("cb")], writes=[bank[2 * sl + h]])

    def EXP2(st):
        gi, sl, c0 = st["gi"], st["gi"] % 2, st["c0"]
        ai = gi % 3
        P.op("act", lambda e, sl=sl, c0=c0, ai=ai: e.activation(out=ATt[ai][:, :, c0:512], in_=ps[:, 2 * sl:2 * sl + 2, c0:512],
                                                                func=AF.Exp, scale=0.125),
             reads=[bank[2 * sl], bank[2 * sl + 1]], writes=[buf("AT%d" % ai)])

    def AV(st):
        gi, qi, kb = st["gi"], st["qi"], st["kb"]
        ai = gi % 3
        ob = 4 + qi % 2
        first = st["m"] == 0
        for sbk in range(max(st["j"], 0), 4):
            for h in range(2):
                P.op("pe", lambda e, sbk=sbk, h=h, ai=ai, ob=ob, kb=kb, first=first, last=st["last"]: e.matmul(
                    ps[:, ob, sbk * 128 + h * 64:sbk * 128 + h * 64 + 64], lhsT=ATt[ai][:, h, sbk * 128:(sbk + 1) * 128],
                    rhs=Vsb[:, kb * 128 + h * 64:kb * 128 + h * 64 + 64], start=first, stop=last, skip_group_check=True),
                    reads=[buf("AT%d" % ai), buf("Vsb")], writes=[bank[ob]])
                first = False

    def EPI_LOAD(qi):
        gs_ = qi % 2
        P.dma("sp", lambda e, qi=qi, gs_=gs_: e.dma_start(
            out=gt[gs_], in_=gsb_d[qi * 512:(qi + 1) * 512, :].rearrange("(s p) f -> p s f", p=128)),
            writes=[buf("gt%d" % gs_)])

    def EPI_A(qi, row0, gsrc):
        gs_ = qi % 2
        ob = 4 + qi % 2
        P.op("dve", lambda e, gs_=gs_, ob=ob: e.tensor_tensor(out=mix[gs_], in0=ps[:, ob, :].rearrange("p (s f) -> p s f", s=4),
                                                              in1=gt[gs_], op=ALU.mult),
             reads=[bank[ob], buf("gt%d" % gs_)], writes=[buf("mix%d" % gs_)])

    def EPI_B(qi, row0):
        gs_ = qi % 2
        for sbk in range(4):
            P.op("pe", lambda e, sbk=sbk, gs_=gs_: e.transpose(tpb[:, sbk * 128:(sbk + 1) * 128], mix[gs_][:, sbk, :], ident),
                 reads=[buf("mix%d" % gs_), buf("cb")], writes=[bank[6]])
        P.op("dve", lambda e, gs_=gs_: e.tensor_copy(out=mixT[gs_], in_=tpb[:, 0:512]),
             reads=[bank[6]], writes=[buf("mixT%d" % gs_)])
        c_ = ch_of[qi]
        o_ = (qi - CH[c_][0]) * 512
        t = P.dma("sp", lambda e, gs_=gs_, c_=c_, o_=o_, row0=row0: e.dma_start(
            out=mixT_q[c_][row0:row0 + 128, o_:o_ + 512], in_=mixT[gs_]),
            reads=[buf("mixT%d" % gs_)])
        mix_tickets[c_].append(t)

    p5_queue = []
    p5_hook = {}

    def run_sb():
        pending = []
        QK(steps[0])
        if nst > 1:
            QK(steps[1])
        EXP1(steps[0])
        LN(steps[0])
        LSUM(steps[0])
        EPI_LOAD(0)
        for i in range(nst):
            st = steps[i]
            TRI(st)
            if i >= 1:
                pv = steps[i - 1]
                AV(pv)
                if pv["last"]:
                    EPI_A(pv["qi"], 0, gsb_d)
                    pending.append(pv["qi"])
                    if pv["qi"] + 1 < DBG.get('sb_tiles', NT):
                        EPI_LOAD(pv["qi"] + 1)
            if p5_queue and p5_queue[0][0] <= i and i % DBG.get('p5_every', 7) == 0 and not pending:
                p5_hook["blk"](p5_queue.pop(0)[1])
            if i + 1 < nst:
                EXP1(steps[i + 1])
            EXP2(st)
            if i + 1 < nst:
                LN(steps[i + 1])
                LSUM(steps[i + 1])
            if pending and not (i >= 1 and steps[i - 1]["last"]):
                qd = pending.pop(0)
                EPI_B(qd, 0)
                if qd in ch_end:
                    AGQ(ch_end[qd], i)
            if i + 2 < nst:
                QK(steps[i + 2])
            if i + 1 < nst:
                CARRY(steps[i + 1])
        AV(steps[nst - 1])
        EPI_A(steps[nst - 1]["qi"], 0, gsb_d)
        pending.append(steps[nst - 1]["qi"])
        while pending:
            qd = pending.pop(0)
            EPI_B(qd, 0)
            if qd in ch_end:
                AGQ(ch_end[qd], nst)

    def AGQ(q, step_i):
        if phase <= 4:
            return
        P.coll(lambda e, q=q: e.collective_compute("AllGather", ALU.bypass, replica_groups=[[0, 1, 2, 3], [4, 5, 6, 7]],
                                                   ins=[mixT_q[q].opt()], outs=[gath_q[q].opt()]),
               writes=[buf("gath%d" % q)], extra=list(mix_tickets[q]))
        for b_ in range(CH[q][0] * 4, CH[q][1] * 4):
            p5_queue.append((step_i + DBG.get('p5_delay', 60), b_))

    ETt = [scr[:, 8704 + i * 512:8704 + (i + 1) * 512].bitcast(BF16).rearrange("p (h t) -> p h t", h=2) for i in range(3)]
    gtd = [scr[:, 10240 + i * 512:10240 + (i + 1) * 512].rearrange("p (s f) -> p s f", s=4) for i in range(2)]
    t1 = [scr[:, 11264 + i * 128:11264 + (i + 1) * 128] for i in range(2)]
    dd = [scr[:, 11520 + i * 512:11520 + (i + 1) * 512].rearrange("p (s f) -> p s f", s=4) for i in range(2)]
    mixd = [scr[:, 12544 + i * 256:12544 + (i + 1) * 256].bitcast(BF16).rearrange("p (s f) -> p s f", s=4) for i in range(2)]
    mixTd = [scr[:, 13056 + i * 256:13056 + (i + 1) * 256].bitcast(BF16) for i in range(2)]
    stat = [scr[:, 13568 + i * 32:13568 + (i + 1) * 32] for i in range(2)]
    junkd = scr[:, 13696:13824]
    tpd = ps[:, 7, :].bitcast(BF16)

    def oreg(mp, sbk):
        r = mp * 4 + sbk
        return 4 + r // 3, (r % 3) * 132

    def df_steps():
        out = []
        for qi in range(DBG.get('df_tiles', NT)):
            for m in range(4 * qi + 4):
                kb = 4 * qi + 3 - m
                j = kb - 4 * qi
                out.append(dict(qi=qi, m=m, kb=kb, j=j, c0=128 * max(j, 0), last=(kb == 0)))
        for gi, st in enumerate(out):
            st["gi"] = gi
        return out

    dsteps = df_steps()
    nds = len(dsteps)

    def QKD(st):
        sl = st["gi"] % 2
        c0, kb, qi = st["c0"], st["kb"], st["qi"]
        diag = st["j"] >= 0
        for mp in range(2):
            P.op("pe", lambda e, mp=mp, sl=sl, c0=c0, kb=kb, qi=qi, diag=diag: e.matmul(
                ps[:, 2 * sl + mp, c0:512], lhsT=KdfT[64 * mp:64 * mp + 64, kb * 128:(kb + 1) * 128],
                rhs=QdfT[64 * mp:64 * mp + 64, qi * 512 + c0:(qi + 1) * 512], start=True, stop=(not diag), skip_group_check=True),
                reads=[buf("QT2"), buf("QT3")], writes=[bank[2 * sl + mp]])
        if diag:
            for mp in range(2):
                P.op("pe", lambda e, mp=mp, sl=sl, c0=c0: e.matmul(
                    ps[:, 2 * sl + mp, c0:c0 + 128], lhsT=ident, rhs=negdf, start=False, stop=True, skip_group_check=True),
                    reads=[buf("cb")], writes=[bank[2 * sl + mp]])

    def EXPD(st):
        gi, sl, c0, m = st["gi"], st["gi"] % 2, st["c0"], st["m"]
        ai = gi % 3
        P.op("act", lambda e, c0=c0, m=m, sl=sl, ai=ai: e.activation(
            out=ETt[ai][:, :, c0:512], in_=ps[:, 2 * sl:2 * sl + 2, c0:512], func=AF.Exp, scale=0.125,
            bias=cf[:, CF_ALI + m:CF_ALI + m + 1]),
            reads=[bank[2 * sl], bank[2 * sl + 1], buf("cf")], writes=[buf("ET%d" % ai)])

    started = set()

    def AVD(st):
        gi, qi, kb = st["gi"], st["qi"], st["kb"]
        ai = gi % 3
        if st["m"] == 0:
            started.clear()
        for mp in range(2):
            for sbk in range(max(st["j"], 0), 4):
                bk, off = oreg(mp, sbk)
                first = bk not in started
                started.add(bk)
                P.op("pe", lambda e, mp=mp, sbk=sbk, ai=ai, bk=bk, off=off, kb=kb, first=first, last=st["last"]: e.matmul(
                    ps[:, bk, off:off + 129], lhsT=ETt[ai][:, mp, sbk * 128:(sbk + 1) * 128],
                    rhs=Vdf3[:, kb, 0:129], start=first, stop=last, skip_group_check=True),
                    reads=[buf("ET%d" % ai), buf("Vdf")], writes=[bank[bk]])

    def EPID_LOAD(qi):
        gs_ = qi % 2
        P.dma("sp", lambda e, qi=qi, gs_=gs_: e.dma_start(
            out=gtd[gs_], in_=gdf_d[qi * 512:(qi + 1) * 512, :].rearrange("(s p) f -> p s f", p=128)),
            writes=[buf("gtd%d" % gs_)])

    Ocp = scr[:, 0:1188].rearrange("p (b c) -> p b c", b=3)
    ocp_bufs = [buf("e0"), buf("e1")]

    def ocp(mp, sbk):
        r = mp * 4 + sbk
        return r // 3, (r % 3) * 132

    def EPID_A0(qi):
        for b_ in range(3):
            P.op("dve", lambda e, b_=b_: e.tensor_copy(out=Ocp[:, b_, :], in_=ps[:, 4 + b_, 0:396]),
                 reads=[bank[4 + b_]], writes=ocp_bufs)

    def EPID_A1(qi):
        gs_ = qi % 2
        stt = stat[gs_]
        for sbk in range(4):
            b1, o1 = ocp(0, sbk)
            b2, o2 = ocp(1, sbk)
            ts = sbk % 2
            P.op("dve", lambda e, b1=b1, o1=o1, sbk=sbk, stt=stt: e.reciprocal(out=stt[:, sbk:sbk + 1], in_=Ocp[:, b1, o1 + 128:o1 + 129]),
                 reads=ocp_bufs, writes=[buf("stat%d" % gs_)])
            P.op("dve", lambda e, b2=b2, o2=o2, sbk=sbk, stt=stt: e.reciprocal(out=stt[:, 4 + sbk:5 + sbk], in_=Ocp[:, b2, o2 + 128:o2 + 129]),
                 reads=ocp_bufs, writes=[buf("stat%d" % gs_)])
            P.op("dve", lambda e, sbk=sbk, stt=stt: e.tensor_tensor(out=stt[:, 8 + sbk:9 + sbk], in0=stt[:, 4 + sbk:5 + sbk], in1=nlam, op=ALU.mult),
                 reads=[buf("stat%d" % gs_), buf("nlam")], writes=[buf("statb%d" % gs_)])
            P.op("dve", lambda e, b1=b1, o1=o1, sbk=sbk, stt=stt, ts=ts: e.tensor_scalar(
                out=t1[ts], in0=Ocp[:, b1, o1:o1 + 128], scalar1=stt[:, sbk:sbk + 1], scalar2=None, op0=ALU.mult),
                reads=ocp_bufs + [buf("stat%d" % gs_)], writes=[buf("t1_%d" % ts)])
            P.op("dve", lambda e, b2=b2, o2=o2, sbk=sbk, stt=stt, ts=ts, gs_=gs_: e.scalar_tensor_tensor(
                out=dd[gs_][:, sbk, :], in0=Ocp[:, b2, o2:o2 + 128], scalar=stt[:, 8 + sbk:9 + sbk], in1=t1[ts],
                op0=ALU.mult, op1=ALU.add),
                reads=ocp_bufs + [buf("statb%d" % gs_), buf("t1_%d" % ts)], writes=[buf("dd%d" % gs_)])

    def EPID_A2(qi):
        gs_ = qi % 2
        stt = stat[gs_]
        for sbk in range(4):
            P.op("act", lambda e, sbk=sbk, stt=stt, gs_=gs_: e.activation(out=junkd, in_=dd[gs_][:, sbk, :], func=AF.Square,
                                                                          accum_out=stt[:, 12 + sbk:13 + sbk]),
                 reads=[buf("dd%d" % gs_)], writes=[buf("junkd"), buf("ssd%d" % gs_)])
        P.op("act", lambda e, stt=stt: e.activation(out=stt[:, 16:20], in_=stt[:, 12:16], func=AF.Ln, scale=1.0 / 128.0, bias=float(EPS)),
             reads=[buf("ssd%d" % gs_)], writes=[buf("rstd%d" % gs_)])
        P.op("act", lambda e, stt=stt: e.activation(out=stt[:, 16:20], in_=stt[:, 16:20], func=AF.Exp, scale=-0.5),
             reads=[buf("rstd%d" % gs_)], writes=[buf("rstd%d" % gs_)])

    def EPID_A3(qi):
        gs_ = qi % 2
        stt = stat[gs_]
        for sbk in range(4):
            P.op("dve", lambda e, sbk=sbk, stt=stt, gs_=gs_: e.scalar_tensor_tensor(
                out=mixd[gs_][:, sbk, :], in0=dd[gs_][:, sbk, :], scalar=stt[:, 16 + sbk:17 + sbk], in1=gtd[gs_][:, sbk, :],
                op0=ALU.mult, op1=ALU.mult),
                reads=[buf("dd%d" % gs_), buf("rstd%d" % gs_), buf("gtd%d" % gs_)], writes=[buf("mixd%d" % gs_)])

    def EPID_B(qi):
        gs_ = qi % 2
        for sbk in range(4):
            P.op("pe", lambda e, sbk=sbk, gs_=gs_: e.transpose(tpd[:, sbk * 128:(sbk + 1) * 128], mixd[gs_][:, sbk, :], ident),
                 reads=[buf("mixd%d" % gs_), buf("cb")], writes=[bank[7]])
        P.op("dve", lambda e, gs_=gs_: e.tensor_copy(out=mixTd[gs_], in_=tpd[:, 0:512]),
             reads=[bank[7]], writes=[buf("mixTd%d" % gs_)])
        c_ = ch_of[qi]
        o_ = (qi - CH[c_][0]) * 512
        t = P.dma("sp", lambda e, gs_=gs_, c_=c_, o_=o_: e.dma_start(
            out=mixT_q[c_][128:256, o_:o_ + 512], in_=mixTd[gs_]),
            reads=[buf("mixTd%d" % gs_)])
        mix_tickets[c_].append(t)

    def run_df():
        pend = []
        stages = [EPID_A1, EPID_A2, EPID_A3, EPID_B]
        delays = [0, 7, 2, 1]

        def advance(force=False):
            for it in list(pend):
                if it[0] > 0 and not force:
                    it[0] -= 1
                    continue
                stages[it[1]](it[2])
                it[1] += 1
                if it[1] == len(stages):
                    pend.remove(it)
                else:
                    it[0] = min(delays[it[1]], 2) if it[2] == 0 else delays[it[1]]

        EPID_LOAD(0)
        QKD(dsteps[0])
        for i in range(nds):
            st = dsteps[i]
            if i + 1 < nds:
                QKD(dsteps[i + 1])
            EXPD(st)
            if i >= 1:
                pv = dsteps[i - 1]
                AVD(pv)
                if pv["last"]:
                    EPID_A0(pv["qi"])
                    pend.append([0, 0, pv["qi"]])
                    if pv["qi"] + 1 < DBG.get('df_tiles', NT):
                        EPID_LOAD(pv["qi"] + 1)
            advance()
        AVD(dsteps[nds - 1])
        EPID_A0(dsteps[nds - 1]["qi"])
        pend.append([0, 0, dsteps[nds - 1]["qi"]])
        while pend:
            advance(force=True)

    mall = [scr[:, 13824 + i * 1024:13824 + (i + 1) * 1024].bitcast(BF16).rearrange("p (k t) -> p k t", k=8) for i in range(2)]
    woutb = scr[:, 15872:16896].bitcast(BF16).rearrange("p (k n) -> p k n", k=8)
    wos = [scr[:, 16896 + i * 256:16896 + (i + 1) * 256] for i in range(2)]
    xqt = [scr[:, 17408 + i * 256:17408 + (i + 1) * 256] for i in range(2)]
    ot = [scr[:, 17920 + i * 256:17920 + (i + 1) * 256] for i in range(2)]
    gath_v = [g.rearrange("(k p) t -> p k t", p=128) for g in gath_q]
    p5ps = ps[:, 7, 256:512]

    def p5_prep():
        for kc in range(8):
            sl = kc % 2
            P.dma("sp", lambda e, kc=kc, sl=sl: e.dma_start(out=wos[sl], in_=wout_d[kc * 128:(kc + 1) * 128, :]), writes=[buf("wos%d" % sl)])
            P.op("dve", lambda e, kc=kc, sl=sl: e.tensor_copy(out=woutb[:, kc, :], in_=wos[sl]), reads=[buf("wos%d" % sl)], writes=[buf("woutb")])

    p5_loaded = set()

    def p5_load(blk):
        if blk in p5_loaded:
            return
        p5_loaded.add(blk)
        q = ch_of[blk // 4]
        lb_ = blk - CH[q][0] * 4
        ms = (blk // 2) % 2
        xs = blk % 2
        if blk % 2 == 0:
            P.dma("sp", lambda e, q=q, lb_=lb_, ms=ms: e.dma_start(out=mall[ms], in_=gath_v[q][:, :, lb_ * 128:lb_ * 128 + 256]),
                  reads=[buf("gath%d" % q)], writes=[buf("mall%d" % ms)])
        P.dma("sp", lambda e, blk=blk, xs=xs: e.dma_start(out=xqt[xs], in_=xq_d[blk * 128:(blk + 1) * 128, :]),
              writes=[buf("xqt%d" % xs)])

    def p5_block(blk):
        ms = (blk // 2) % 2
        xs = blk % 2
        os_ = blk % 2
        p5_load(blk)
        tb = blk % 2
        for kc in range(8):
            P.op("pe", lambda e, kc=kc, ms=ms, tb=tb: e.matmul(
                p5ps, lhsT=mall[ms][:, kc, tb * 128:(tb + 1) * 128], rhs=woutb[:, kc, :],
                start=(kc == 0), stop=(kc == 7)),
                reads=[buf("mall%d" % ms), buf("woutb")], writes=[bank[7]])
        P.op("dve", lambda e, os_=os_: e.tensor_tensor(out=ot[os_], in0=p5ps, in1=gate_rep, op=ALU.mult),
             reads=[bank[7], buf("gate")], writes=[buf("ot%d" % os_)])
        P.op("pool", lambda e, os_=os_, xs=xs: e.tensor_tensor(out=ot[os_], in0=ot[os_], in1=xqt[xs], op=ALU.add),
             reads=[buf("xqt%d" % xs)], writes=[buf("ot%d" % os_)])
        P.dma("pool", lambda e, blk=blk, os_=os_: e.dma_start(out=out_d[blk * 128:(blk + 1) * 128, :], in_=ot[os_]),
              reads=[buf("ot%d" % os_)])
        for (_, nb_) in p5_queue[:2]:
            if nb_ // 2 <= blk // 2 + 1:
                p5_load(nb_)

    p5_hook["blk"] = p5_block

    run_df()
    if phase == 4:
        return _finish(nc, P, B, dbg, locals())
    p5_prep()
    run_sb()
    while p5_queue:
        p5_block(p5_queue.pop(0)[1])
    return _finish(nc, P, B, dbg, locals())


def _finish(nc, P, B, dbg, L):
    P.barrier()
    P.emit(nc)
    return nc


def _col_index(hg):
    r = np.arange(128)
    sbo = 128 * hg + r
    return np.concatenate([sbo, 512 + sbo, 2048 + sbo, 2560 + sbo,
                           1024 + sbo, 1536 + sbo, 3072 + sbo, 3584 + sbo])


def make_in_maps(x, c, norm_g, w_ada, b_ada, w_in, q_norm_g, k_norm_g,
                 lambda_q1, lambda_k1, lambda_q2, lambda_k2, subln_g, w_out):
    f = np.float32
    x = np.asarray(x, f); c = np.asarray(c, f)
    w_ada = np.asarray(w_ada, f)[0]; b_ada = np.asarray(b_ada, f)[0]
    w_in = np.asarray(w_in, f)[0]; w_out = np.asarray(w_out, f)[0]
    norm_g = np.asarray(norm_g, f)[0]
    qg = np.asarray(q_norm_g, f)[0]; kg = np.asarray(k_norm_g, f)[0]
    lamv = np.concatenate([np.asarray(a, f)[0] for a in (lambda_q1, lambda_k1, lambda_q2, lambda_k2)])
    subg = np.asarray(subln_g, f)[0]
    wss = np.ascontiguousarray(w_ada[:, 0:2048])
    bcol = np.ascontiguousarray(b_ada[0:2048].reshape(16, 128).T)
    ngcol = np.ascontiguousarray(norm_g.reshape(8, 128).T)
    qkg = np.ascontiguousarray(np.stack([np.concatenate([qg, qg]), np.concatenate([kg, kg])], axis=1))
    rows = np.concatenate([np.concatenate([128 * r + np.arange(128), 512 + 128 * r + np.arange(128)]) for r in range(4)])
    w_out_p = w_out[rows]
    maps = []
    for core in range(8):
        b, hg = core // 4, core % 4
        cb, cf = _consts(hg)
        maps.append({
            "x": np.ascontiguousarray(x[b]),
            "xq": np.ascontiguousarray(x[b][:, 256 * hg:256 * (hg + 1)]),
            "ccol": np.ascontiguousarray(c[b].reshape(8, 128).T),
            "ngcol": ngcol, "bcol": bcol, "wss": wss,
            "wg": np.ascontiguousarray(w_ada[:, 2048 + 256 * hg:2048 + 256 * (hg + 1)]),
            "bg": np.ascontiguousarray(b_ada[2048 + 256 * hg:2048 + 256 * (hg + 1)]),
            "win": np.ascontiguousarray(w_in[:, _col_index(hg)]),
            "qkg": qkg, "lamv": lamv, "subg": subg,
            "wout": np.ascontiguousarray(w_out_p[:, 256 * hg:256 * (hg + 1)]),
            "cb": cb, "cf": cf,
        })
    return maps


_NC_CACHE = {}


def kernel(**inputs):
    in_maps = make_in_maps(**inputs)
    if "nc" not in _NC_CACHE:
        _NC_CACHE["nc"] = build()
    res = run_bass_kernel_spmd(_NC_CACHE["nc"], in_maps, core_ids=list(range(8)))
    out = np.empty((2, S, D), np.float32)
    for core in range(8):
        b, hg = core // 4, core % 4
        out[b][:, 256 * hg:256 * (hg + 1)] = res.results[core]["out"]
    return out
```

```python
import numpy as np
import ml_dtypes
import concourse.bass as bass
import concourse.mybir as mybir
from concourse.bass_utils import run_bass_kernel_spmd

F32 = mybir.dt.float32
BF16 = mybir.dt.bfloat16
AF = mybir.ActivationFunctionType
ALU = mybir.AluOpType
AX = mybir.AxisListType

S = 8192
D = 1024
NT = 16
NB = 64
EPS = 1e-6
LAM_INIT = 0.2
NEG = -30000.0
DBG = {}

ENGS = ("pe", "act", "dve", "pool", "sp")
NDSEM = 8


class Buf:
    __slots__ = ("w", "r", "name")

    def __init__(self, name=""):
        self.w = None
        self.r = {}
        self.name = name


class Prog:
    def __init__(self):
        self.q = {e: [] for e in ENGS}
        self.cnt = {e: 0 for e in ENGS}
        self.seen = {e: {} for e in ENGS}
        self.dma_n = {"sp": 0, "pool": 0, "act": 0}
        self.dma_last = {}
        self.cc_n = 0

    def _wait(self, eng, t, raw=False):
        key, val = t
        if key == eng:
            if eng == "pe" or (val < self.cnt[eng] and not raw):
                return
        if self.seen[eng].get(key, 0) >= val:
            return
        self.seen[eng][key] = val
        self.q[eng].append(("w", key, val))

    def _deps(self, eng, reads, writes, extra):
        deps = set(extra)
        rawd = set()
        for b in reads:
            if b.w is not None:
                rawd.add(b.w)
        for b in writes:
            if b.w is not None:
                deps.add(b.w)
            deps.update(b.r.values())
        for t in sorted(rawd):
            self._wait(eng, t, raw=True)
        for t in sorted(deps - rawd):
            self._wait(eng, t)

    def _mark(self, t, reads, writes):
        for b in reads:
            b.r[t[0]] = t
        for b in writes:
            b.w = t
            b.r = {}

    @staticmethod
    def _excl(reads, writes):
        r = [b for b in reads if not b.name.startswith("bank")]
        w = list(writes) + [b for b in reads if b.name.startswith("bank")]
        return r, w

    def op(self, eng, fn, reads=(), writes=(), extra=()):
        reads, writes = self._excl(reads, writes)
        self._deps(eng, reads, writes, extra)
        self.cnt[eng] += 1
        t = (eng, self.cnt[eng])
        self.q[eng].append(("i", fn))
        self._mark(t, reads, writes)
        return t

    def dma(self, queue, fn, reads=(), writes=(), extra=()):
        i = self.dma_n[queue]
        self.dma_n[queue] += 1
        slot = i % NDSEM
        val = 16 * (i // NDSEM + 1)
        key = "d_%s_%d" % (queue, slot)
        if i >= NDSEM:
            self._wait(queue, (key, val - 16))
        self._deps(queue, reads, writes, extra)
        self.q[queue].append(("d", fn, key))
        t = (key, val)
        self.dma_last[key] = t
        self._mark(t, reads, writes)
        return t

    def coll(self, fn, reads=(), writes=(), extra=()):
        self._deps("pool", reads, writes, extra)
        self.cc_n += 1
        self.q["pool"].append(("c", fn))
        t = ("cc", self.cc_n)
        self.dma_last["cc"] = t
        self._mark(t, reads, writes)
        return t

    def barrier(self, skip_cc=False):
        tickets = [(e, self.cnt[e]) for e in ENGS if self.cnt[e] > 0]
        tickets += [t for k, t in self.dma_last.items() if not (skip_cc and k == "cc")]
        for e in ENGS:
            for t in tickets:
                if t[0] != e:
                    self._wait(e, t)

    def sem_keys(self):
        keys = list(ENGS) + ["cc"]
        for qn in ("sp", "pool", "act"):
            for s in range(NDSEM):
                keys.append("d_%s_%d" % (qn, s))
        return keys

    def _signal_maps(self):
        waited = {e: set() for e in ENGS}
        for e in ENGS:
            for it in self.q[e]:
                if it[0] == "w" and it[1] in waited:
                    waited[it[1]].add(it[2])
        self.rank = {}
        for e in ENGS:
            self.rank[e] = {v: i + 1 for i, v in enumerate(sorted(waited[e]))}

    def replay(self, eng, e, sems):
        n = 0
        rank = self.rank
        for it in self.q[eng]:
            if it[0] == "w":
                if it[1] in rank:
                    e.wait_ge(sems[it[1]], rank[it[1]][it[2]])
                else:
                    e.wait_ge(sems[it[1]], it[2])
            elif it[0] == "i":
                n += 1
                ins = it[1](e)
                if n in rank[eng]:
                    ins.then_inc(sems[eng], 1)
            elif it[0] == "c":
                it[1](e).then_inc(sems["cc"], 1)
            else:
                it[1](e).then_inc(sems[it[2]], 16)

    def emit(self, nc):
        from contextlib import ExitStack
        self._signal_maps()
        with ExitStack() as st:
            sems = {}
            for k in self.sem_keys():
                sems[k] = st.enter_context(nc.semaphore(k))
            block = st.enter_context(nc.Block())

            @block.tensor
            def _(e):
                self.replay("pe", e, sems)

            @block.scalar
            def _(e):
                self.replay("act", e, sems)

            @block.vector
            def _(e):
                self.replay("dve", e, sems)

            @block.gpsimd
            def _(e):
                self.replay("pool", e, sems)

            @block.sync
            def _(e):
                self.replay("sp", e, sems)


CB_ID, CB_TRI, CB_ONES, CB_NSB, CB_NDF, CB_SEL = 0, 128, 256, 384, 512, 640
CB_W = 768
CF_BLK, CF_ALI = 0, 128
CF_W = 192


def _consts(hg):
    cb = np.zeros((128, CB_W), np.float32)
    i = np.arange(128)
    cb[:, CB_ID:CB_ID + 128] = np.eye(128)
    cb[:, CB_TRI:CB_TRI + 128] = np.where(i[:, None] >= i[None, :], -8.0, 0.0)
    cb[:, CB_ONES:CB_ONES + 128] = -8.0
    cb[:, CB_NSB:CB_NSB + 128] = np.where(i[:, None] < i[None, :], 0.0, NEG)
    cb[:, CB_NDF:CB_NDF + 128] = np.where(i[:, None] <= i[None, :], 0.0, NEG)
    cb[0, CB_SEL:CB_SEL + 128] = 1.0
    cb[32, CB_SEL:CB_SEL + 128] = 1.0
    cf = np.zeros((128, CF_W), np.float32)
    cf[:, CF_BLK:CF_BLK + 128] = (i[:, None] // 64 == i[None, :] // 64)
    slope = 2.0 ** (-2.0 * (hg + 1))
    m = np.arange(64)
    cf[:, CF_ALI:CF_ALI + 64] = slope * (i[:, None] + 129.0 - 128.0 * m[None, :])
    return cb.astype(ml_dtypes.bfloat16), cf


def build(phase=9, debug=False):
    nc = bass.Bass("TRN2", target_bir_lowering=False)
    P = Prog()

    def din(name, shape, dt=F32):
        return nc.dram_tensor(name, list(shape), dt, kind="ExternalInput").ap()

    x_d = din("x", [S, D])
    xq_d = din("xq", [S, 256])
    ccol_d = din("ccol", [128, 8])
    ngcol_d = din("ngcol", [128, 8])
    bcol_d = din("bcol", [128, 16])
    wss_d = din("wss", [D, 2048])
    wg_d = din("wg", [D, 256])
    bg_d = din("bg", [256])
    win_d = din("win", [D, 1024])
    qkg_d = din("qkg", [128, 2])
    lamv_d = din("lamv", [256])
    subg_d = din("subg", [128])
    wout_d = din("wout", [D, 256])
    cb_d = din("cb", [128, CB_W], BF16)
    cf_d = din("cf", [128, CF_W])
    out_d = nc.dram_tensor("out", [S, 256], F32, kind="ExternalOutput").ap()

    gsb_d = nc.dram_tensor("gsb_s", [S, 128], F32).ap()
    gdf_d = nc.dram_tensor("gdf_s", [S, 128], F32).ap()
    CH = [(0, 4), (4, 8), (8, 12), (12, 15), (15, 16)]
    NCH = len(CH)
    ch_of = {qi: c for c, (a, b_) in enumerate(CH) for qi in range(a, b_)}
    ch_end = {b_ - 1: c for c, (a, b_) in enumerate(CH)}
    mixT_q = [nc.dram_tensor("mixT_s%d" % c, [256, (b_ - a) * 512], BF16).ap() for c, (a, b_) in enumerate(CH)]
    gath_q = [nc.dram_tensor("gath_s%d" % c, [1024, (b_ - a) * 512], BF16).ap() for c, (a, b_) in enumerate(CH)]
    mix_tickets = [[] for _ in range(NCH)]

    def sb(name, shape, dt):
        return nc.alloc_sbuf_tensor("sb_" + name, list(shape), dt).ap()

    QsbT = sb("QsbT", [128, S], BF16)
    KsbT = sb("KsbT", [128, S], BF16)
    QdfT = sb("QdfT", [128, S], BF16)
    KdfT = sb("KdfT", [128, S], BF16)
    Vsb = sb("Vsb", [128, NB * 128], BF16)
    Vdf = sb("Vdf", [128, NB * 130], BF16)
    cb = sb("cb", [128, CB_W], BF16)
    cf = sb("cf", [128, CF_W], F32)
    small = sb("small", [128, 1024], F32)
    scr = sb("scr", [128, 18432], F32)
    ps = nc.alloc_psum_tensor("ps", [128, 8, 512], F32).ap()

    ident = cb[:, CB_ID:CB_ID + 128]
    tri = cb[:, CB_TRI:CB_TRI + 128]
    onesm8 = cb[:, CB_ONES:CB_ONES + 128]
    negsb = cb[:, CB_NSB:CB_NSB + 128]
    negdf = cb[:, CB_NDF:CB_NDF + 128]
    sel = cb[:, CB_SEL:CB_SEL + 128]
    blockones = cf[:, CF_BLK:CF_BLK + 128]

    ccol = small[:, 0:8]
    ngcol = small[:, 8:16]
    bcol = small[:, 16:32]
    qkg = small[:, 32:34]
    g8 = small[:, 34:36]
    shiftT = small[:, 36:44]
    Acol = small[:, 44:52]
    biasT = small[:, 52:56]
    lam_t = small[:, 56:60]
    ccol2 = small[:, 64:80]
    shift2 = small[:, 80:96]
    ss_x = small[:, 96:160]
    r_x = small[:, 160:224]
    gate_rep = small[:, 256:512]
    subg8 = small[:, 512:640]
    lamv = small[:, 640:896]
    tmp_s = small[:, 896:1024]

    B = {}

    def buf(n):
        if n not in B:
            B[n] = Buf(n)
        return B[n]

    bank = [buf("bank%d" % i) for i in range(8)]

    P.dma("sp", lambda e: e.dma_start(out=cb, in_=cb_d), writes=[buf("cb")])
    P.dma("sp", lambda e: e.dma_start(out=cf, in_=cf_d), writes=[buf("cf")])
    P.dma("sp", lambda e: e.dma_start(out=ccol, in_=ccol_d), writes=[buf("vec")])
    P.dma("sp", lambda e: e.dma_start(out=ngcol, in_=ngcol_d), writes=[buf("vec")])
    P.dma("sp", lambda e: e.dma_start(out=bcol, in_=bcol_d), writes=[buf("vec")])
    P.dma("sp", lambda e: e.dma_start(out=qkg, in_=qkg_d), writes=[buf("vec")])
    P.dma("sp", lambda e: e.dma_start(out=lamv, in_=lamv_d.partition_broadcast(128)), writes=[buf("vec")])
    P.dma("sp", lambda e: e.dma_start(out=subg8, in_=subg_d.partition_broadcast(128)), writes=[buf("vec")])
    P.dma("sp", lambda e: e.dma_start(out=gate_rep, in_=bg_d.partition_broadcast(128)), writes=[buf("gate")])

    Vdf3 = Vdf.rearrange("p (k c) -> p k c", c=130)
    P.op("pool", lambda e: e.memset(Vdf3[:, :, 128:130], 1.0), writes=[buf("Vdf")])

    cc2 = ccol2.rearrange("p (k c) -> p k c", c=2)
    for j in range(2):
        P.op("dve", lambda e, j=j: e.tensor_copy(out=cc2[:, :, j], in_=ccol), reads=[buf("vec")], writes=[buf("cc2")])
    P.op("dve", lambda e: e.tensor_scalar(out=subg8, in0=subg8, scalar1=1.0 - LAM_INIT, scalar2=None, op0=ALU.mult),
         reads=[buf("vec")], writes=[buf("subg8")])
    P.op("dve", lambda e: e.tensor_scalar(out=g8, in0=qkg, scalar1=8.0, scalar2=None, op0=ALU.mult),
         reads=[buf("vec")], writes=[buf("g8")])
    P.op("dve", lambda e: e.tensor_tensor(out=tmp_s[:, 0:64], in0=lamv[:, 0:64], in1=lamv[:, 64:128], op=ALU.mult),
         reads=[buf("vec")], writes=[buf("tmp_s")])
    P.op("dve", lambda e: e.tensor_tensor(out=tmp_s[:, 64:128], in0=lamv[:, 128:192], in1=lamv[:, 192:256], op=ALU.mult),
         reads=[buf("vec")], writes=[buf("tmp_s")])
    P.op("dve", lambda e: e.reduce_sum(out=lam_t[:, 0:2], in_=tmp_s.rearrange("p (a b) -> p a b", a=2), axis=AX.X),
         reads=[buf("tmp_s")], writes=[buf("lam")])
    P.op("act", lambda e: e.activation(out=lam_t[:, 0:2], in_=lam_t[:, 0:2], func=AF.Exp),
         reads=[buf("lam")], writes=[buf("lam")])
    P.op("dve", lambda e: e.tensor_tensor(out=lam_t[:, 2:3], in0=lam_t[:, 0:1], in1=lam_t[:, 1:2], op=ALU.subtract),
         reads=[buf("lam")], writes=[buf("lam2")])
    P.op("dve", lambda e: e.tensor_scalar(out=lam_t[:, 3:4], in0=lam_t[:, 2:3], scalar1=LAM_INIT, scalar2=-1.0,
                                          op0=ALU.add, op1=ALU.mult),
         reads=[buf("lam2")], writes=[buf("nlam")])
    nlam = lam_t[:, 3:4]

    wst = [scr[:, i * 4096:(i + 1) * 4096] for i in range(2)]
    wss_v = wss_d.rearrange("(k p) n -> p k n", p=128)
    modps = ps[:, 7, 0:32].rearrange("p (n c) -> p n c", c=2)
    first = True
    for pc in range(4):
        sl = pc % 2
        w3 = wst[sl].rearrange("p (k n) -> p k n", k=8)
        P.dma("sp", lambda e, w3=w3, pc=pc: e.dma_start(out=w3, in_=wss_v[:, :, pc * 512:(pc + 1) * 512]),
              writes=[buf("wst%d" % sl)])
        for nb in range(4):
            for kc in range(8):
                P.op("pe", lambda e, w3=w3, nb=nb, kc=kc, pc=pc, st=first: e.matmul(
                    modps[:, pc * 4 + nb, :], lhsT=w3[:, kc, nb * 128:(nb + 1) * 128], rhs=cc2[:, kc, :],
                    start=st, stop=(kc == 7), skip_group_check=True),
                    reads=[buf("wst%d" % sl), buf("cc2")], writes=[bank[7]])
                first = False
    P.op("dve", lambda e: e.tensor_tensor(out=shiftT, in0=modps[:, 0:8, 0], in1=bcol[:, 0:8], op=ALU.add),
         reads=[bank[7], buf("vec")], writes=[buf("shiftT")])
    P.op("dve", lambda e: e.tensor_tensor(out=Acol, in0=modps[:, 8:16, 0], in1=bcol[:, 8:16], op=ALU.add),
         reads=[bank[7], buf("vec")], writes=[buf("Acol")])
    P.op("dve", lambda e: e.tensor_scalar(out=Acol, in0=Acol, scalar1=1.0, scalar2=32.0, op0=ALU.add, op1=ALU.mult),
         reads=[buf("Acol")], writes=[buf("Acol")])
    P.op("pool", lambda e: e.tensor_tensor(out=Acol, in0=Acol, in1=ngcol, op=ALU.mult),
         reads=[buf("Acol"), buf("vec")], writes=[buf("Acol")])
    sh2 = shift2.rearrange("p (k c) -> p k c", c=2)
    for j in range(2):
        P.op("dve", lambda e, j=j: e.tensor_copy(out=sh2[:, :, j], in_=shiftT), reads=[buf("shiftT")], writes=[buf("sh2")])

    crep = scr[:, 8192:9216].rearrange("p (k m) -> p k m", k=8)
    wgst = scr[:, 9216:11264].rearrange("p (k n) -> p k n", k=8)
    P.dma("sp", lambda e: e.dma_start(out=wgst, in_=wg_d.rearrange("(k p) n -> p k n", p=128)), writes=[buf("wgst")])
    for kc in range(8):
        P.op("pool", lambda e, kc=kc: e.tensor_copy(out=crep[:, kc, :], in_=ccol[:, kc:kc + 1].to_broadcast([128, 128])),
             reads=[buf("vec")], writes=[buf("crep")])
    for kc in range(8):
        P.op("pe", lambda e, kc=kc: e.matmul(ps[:, 6, 0:256], lhsT=crep[:, kc, :], rhs=wgst[:, kc, :],
                                             start=(kc == 0), stop=(kc == 7)),
             reads=[buf("crep"), buf("wgst")], writes=[bank[6]])
    P.op("dve", lambda e: e.tensor_tensor(out=gate_rep, in0=ps[:, 6, 0:256], in1=gate_rep, op=ALU.add),
         reads=[bank[6], buf("gate")], writes=[buf("gate")])

    Wp = scr[:, 11264:15360].bitcast(BF16).rearrange("p (k n) -> p k n", k=8)
    wst2 = [scr[:, i * 1024:(i + 1) * 1024] for i in range(2)]
    win_v = win_d.rearrange("(k p) n -> p k n", p=128)
    bTps = ps[:, 5, 0:8].rearrange("p (g c) -> p g c", c=2)
    browps = ps[:, 4, :]
    shrep = scr[:, 15360:16384].rearrange("p (k m) -> p k m", k=8)
    for kc in range(8):
        P.op("pool", lambda e, kc=kc: e.tensor_copy(out=shrep[:, kc, :], in_=shiftT[:, kc:kc + 1].to_broadcast([128, 128])),
             reads=[buf("shiftT")], writes=[buf("shrep")])
    for kc in range(8):
        sl = kc % 2
        P.dma("sp", lambda e, kc=kc, sl=sl: e.dma_start(out=wst2[sl], in_=win_v[:, kc, :]),
              writes=[buf("w2st%d" % sl)] + ([buf("wst0"), buf("wst1")] if kc < 2 else []))
        for g in range(4):
            P.op("pe", lambda e, kc=kc, sl=sl, g=g: e.matmul(
                bTps[:, g, :], lhsT=wst2[sl][:, g * 128:(g + 1) * 128], rhs=sh2[:, kc, :],
                start=(kc == 0 and g == 0), stop=(kc == 7), skip_group_check=True),
                reads=[buf("w2st%d" % sl), buf("sh2")], writes=[bank[5]])
        P.op("pe", lambda e, kc=kc, sl=sl: e.matmul(
            browps, lhsT=shrep[:, kc, :], rhs=wst2[sl][:, 512:1024], start=(kc == 0), stop=(kc == 7)),
            reads=[buf("w2st%d" % sl), buf("shrep")], writes=[bank[4]])
        P.op("act", lambda e, kc=kc, sl=sl: e.activation(out=Wp[:, kc, :], in_=wst2[sl], func=AF.Copy,
                                                         scale=Acol[:, kc:kc + 1]),
             reads=[buf("w2st%d" % sl), buf("Acol")], writes=[buf("Wp")])
    P.op("dve", lambda e: e.tensor_copy(out=biasT, in_=bTps[:, :, 0]), reads=[bank[5]], writes=[buf("biasT")])
    HL = scr[:, 16384:16640].bitcast(BF16)
    hi_rep = scr[:, 16640:16896].bitcast(BF16)
    brow = scr[:, 16896:17408]
    P.op("pool", lambda e: e.memset(HL, 0.0), writes=[buf("HL")])
    P.op("dve", lambda e: e.tensor_copy(out=brow, in_=browps), reads=[bank[4]], writes=[buf("brow")])
    P.op("dve", lambda e: e.tensor_copy(out=hi_rep, in_=brow), reads=[buf("brow")], writes=[buf("hi_rep")])
    P.op("dve", lambda e: e.tensor_copy(out=HL[0:1, :], in_=hi_rep[0:1, :]), reads=[buf("hi_rep")], writes=[buf("HL")])
    P.op("dve", lambda e: e.tensor_tensor(out=HL[32:33, :], in0=brow[32:33, :], in1=hi_rep[32:33, :], op=ALU.subtract),
         reads=[buf("hi_rep"), buf("brow")], writes=[buf("HL")])

    dbg = {}
    if phase == 0:
        if debug:
            P.barrier()
            d = nc.dram_tensor("dbg_small", [128, 1024], F32, kind="ExternalOutput").ap()
            P.dma("sp", lambda e, d=d: e.dma_start(out=d, in_=small))
            d = nc.dram_tensor("dbg_scr", [128, 18432], F32, kind="ExternalOutput").ap()
            P.dma("sp", lambda e, d=d: e.dma_start(out=d, in_=scr))
        return _finish(nc, P, B, dbg, locals())

    xt = [scr[:, i * 1024:(i + 1) * 1024] for i in range(3)]
    xn = [scr[:, 3072 + i * 512:3072 + (i + 1) * 512].bitcast(BF16) for i in range(2)]
    junk = scr[:, 4096:4608].bitcast(BF16)
    hT = [scr[:, 4608 + i * 2048:4608 + (i + 1) * 2048].bitcast(BF16).rearrange("p (k t) -> p k t", k=8)
          for i in range(2)]
    qb = [scr[:, 8704:9216], scr[:, 17408:17920]]
    sq = [scr[:, 9216 + i * 512:9216 + (i + 1) * 512] for i in range(2)]
    rr = [scr[:, 10240 + i * 512:10240 + (i + 1) * 512] for i in range(2)]
    ge = [scr[:, 15360 + i * 256:15360 + (i + 1) * 256] for i in range(2)]
    gs = [scr[:, 15872 + i * 256:15872 + (i + 1) * 256] for i in range(2)] + \
         [scr[:, 17920 + i * 256:17920 + (i + 1) * 256] for i in range(2)]
    tps = [ps[:, i, :].bitcast(BF16) for i in range(2)]
    pstm = [ps[:, 2 + i, :] for i in range(2)]
    psfm = [ps[:, 4 + i, :] for i in range(2)]
    psss = ps[:, 6, :]
    P.barrier()

    QT = {0: QsbT, 1: KsbT, 2: QdfT, 3: KdfT}
    NBLK = DBG.get('tiles', NT) * 4

    def stA(bi):
        xs, ns = bi % 3, bi % 2
        P.dma("sp", lambda e, bi=bi, xs=xs: e.dma_start(out=xt[xs], in_=x_d[bi * 128:(bi + 1) * 128, :]),
              writes=[buf("xt%d" % xs)])
        P.op("act", lambda e, bi=bi, xs=xs: e.activation(out=junk, in_=xt[xs], func=AF.Square,
                                                         accum_out=ss_x[:, bi:bi + 1]),
             reads=[buf("xt%d" % xs)], writes=[buf("junk"), buf("ssx%d" % ns)])

    def stA2(bi):
        xs, ns = bi % 3, bi % 2
        P.op("act", lambda e, bi=bi: e.activation(out=r_x[:, bi:bi + 1], in_=ss_x[:, bi:bi + 1], func=AF.Ln,
                                                  bias=float(D * EPS)),
             reads=[buf("ssx%d" % ns)], writes=[buf("rx%d" % ns)])

    def stA3(bi):
        xs, ns = bi % 3, bi % 2
        P.op("act", lambda e, bi=bi: e.activation(out=r_x[:, bi:bi + 1], in_=r_x[:, bi:bi + 1], func=AF.Exp,
                                                  scale=-0.5),
             reads=[buf("rx%d" % ns)], writes=[buf("rx%d" % ns)])
        P.op("dve", lambda e, bi=bi, xs=xs, ns=ns: e.tensor_scalar(out=xn[ns], in0=xt[xs], scalar1=r_x[:, bi:bi + 1],
                                                                   scalar2=None, op0=ALU.mult),
             reads=[buf("xt%d" % xs), buf("rx%d" % ns)], writes=[buf("xn%d" % ns)])

    def stB(bi):
        ns, hs, sub = bi % 2, (bi // 4) % 2, bi % 4
        for kc in range(8):
            P.op("pe", lambda e, kc=kc, ns=ns: e.transpose(tps[ns][:, kc * 128:(kc + 1) * 128],
                                                           xn[ns][:, kc * 128:(kc + 1) * 128], ident),
                 reads=[buf("xn%d" % ns), buf("cb")], writes=[bank[ns]])
        P.op("dve", lambda e, ns=ns, hs=hs, sub=sub: e.tensor_copy(
            out=hT[hs][:, :, sub * 128:(sub + 1) * 128], in_=tps[ns].rearrange("p (k t) -> p k t", k=8)),
            reads=[bank[ns]], writes=[buf("hT%d" % hs)])

    def stC1(bi):
        hs, sub, tm = (bi // 4) % 2, bi % 4, bi % 2
        for kc in range(8):
            P.op("pe", lambda e, kc=kc, hs=hs, sub=sub, tm=tm: e.matmul(
                pstm[tm], lhsT=hT[hs][:, kc, sub * 128:(sub + 1) * 128], rhs=Wp[:, kc, 512:1024],
                start=(kc == 0), stop=False),
                reads=[buf("hT%d" % hs), buf("Wp")], writes=[bank[2 + tm]])
        P.op("pe", lambda e, tm=tm: e.matmul(pstm[tm], lhsT=sel, rhs=HL, start=False, stop=True),
             reads=[buf("HL"), buf("cb")], writes=[bank[2 + tm]])

    def stC2(bi):
        tm = bi % 2
        gin = pstm[tm].rearrange("p (a two b) -> p a two b", two=2, b=128)[:, :, 1, :]
        ge3 = ge[tm].rearrange("p (a b) -> p a b", a=2)
        g4 = bi % 4
        gs3 = gs[g4].rearrange("p (a b) -> p a b", a=2)
        gc3 = scr[:, 16640 + tm * 256:16640 + (tm + 1) * 256].rearrange("p (a b) -> p a b", a=2)
        P.op("act", lambda e, gin=gin, ge3=ge3: e.activation(out=ge3, in_=gin, func=AF.Exp, scale=-1.0),
             reads=[bank[2 + tm]], writes=[buf("ge%d" % tm)])
        P.op("act", lambda e, gin=gin, gc3=gc3: e.activation(out=gc3, in_=gin, func=AF.Copy),
             reads=[bank[2 + tm]], writes=[buf("gc%d" % tm)])
        P.op("dve", lambda e, tm=tm, bi=bi: e.tensor_copy(out=Vsb[:, bi * 128:(bi + 1) * 128], in_=pstm[tm][:, 0:128]),
             reads=[bank[2 + tm]], writes=[buf("Vsb")])
        P.op("dve", lambda e, tm=tm, bi=bi: e.tensor_copy(out=Vdf3[:, bi, 0:128], in_=pstm[tm][:, 256:384]),
             reads=[bank[2 + tm]], writes=[buf("Vdf")])
        P.op("act", lambda e, ge3=ge3: e.activation(out=ge3, in_=ge3, func=AF.Ln, bias=1.0),
             reads=[buf("ge%d" % tm)], writes=[buf("ge%d" % tm)])
        P.op("act", lambda e, ge3=ge3: e.activation(out=ge3, in_=ge3, func=AF.Exp, scale=-1.0),
             reads=[buf("ge%d" % tm)], writes=[buf("ge%d" % tm)])
        P.op("dve", lambda e, gc3=gc3, ge3=ge3, gs3=gs3: e.tensor_tensor(out=gs3, in0=gc3, in1=ge3, op=ALU.mult),
             reads=[buf("gc%d" % tm), buf("ge%d" % tm)], writes=[buf("gs%d" % g4)])
        P.op("pool", lambda e, gs3=gs3: e.tensor_tensor(out=gs3[:, 1, :], in0=gs3[:, 1, :], in1=subg8, op=ALU.mult),
             reads=[buf("gs%d" % g4), buf("subg8")], writes=[buf("gs%d" % g4)])
        P.dma("pool", lambda e, gs3=gs3, bi=bi: e.dma_start(out=gsb_d[bi * 128:(bi + 1) * 128, :], in_=gs3[:, 0, :]),
              reads=[buf("gs%d" % g4)])
        P.dma("pool", lambda e, gs3=gs3, bi=bi: e.dma_start(out=gdf_d[bi * 128:(bi + 1) * 128, :], in_=gs3[:, 1, :]),
              reads=[buf("gs%d" % g4)])

    def stD_mm(ti, g):
        hs, fs = ti % 2, g % 2
        for kc in range(8):
            P.op("pe", lambda e, kc=kc, g=g, hs=hs, fs=fs: e.matmul(
                psfm[fs], lhsT=Wp[:, kc, g * 128:(g + 1) * 128], rhs=hT[hs][:, kc, :],
                start=(kc == 0), stop=(kc == 7)),
                reads=[buf("hT%d" % hs), buf("Wp")], writes=[bank[4 + fs]])

    def stD_ev(ti, g):
        fs = g % 2
        dst = QT[g][:, ti * 512:(ti + 1) * 512]
        if g < 2:
            P.op("act", lambda e, fs=fs, dst=dst, g=g: e.activation(out=dst, in_=psfm[fs], func=AF.Identity,
                                                                    bias=biasT[:, g:g + 1]),
                 reads=[bank[4 + fs], buf("biasT")], writes=[buf("QT%d" % g)])
        else:
            qs = g % 2
            P.op("act", lambda e, fs=fs, qs=qs, g=g: e.activation(out=qb[qs], in_=psfm[fs], func=AF.Identity,
                                                                  bias=biasT[:, g:g + 1]),
                 reads=[bank[4 + fs], buf("biasT")], writes=[buf("qb%d" % qs)])
            P.op("act", lambda e, fs=fs, qs=qs, g=g: e.activation(out=sq[qs], in_=psfm[fs], func=AF.Square,
                                                                  bias=biasT[:, g:g + 1]),
                 reads=[bank[4 + fs], buf("biasT")], writes=[buf("sq%d" % qs)])
            P.op("pe", lambda e, qs=qs: e.matmul(ps[:, 6 + qs, :], lhsT=blockones, rhs=sq[qs], start=True, stop=True),
                 reads=[buf("sq%d" % qs), buf("cf")], writes=[bank[6 + qs]])

    def stD_fin(ti, g):
        qs = g % 2
        dst = QT[g][:, ti * 512:(ti + 1) * 512]
        P.op("act", lambda e, qs=qs: e.activation(out=rr[qs], in_=ps[:, 6 + qs, :], func=AF.Ln, bias=float(64 * EPS)),
             reads=[bank[6 + qs]], writes=[buf("rr%d" % qs)])
        P.op("act", lambda e, qs=qs: e.activation(out=rr[qs], in_=rr[qs], func=AF.Exp, scale=-0.5),
             reads=[buf("rr%d" % qs)], writes=[buf("rr%d" % qs)])
        P.op("dve", lambda e, qs=qs, dst=dst, g=g: e.scalar_tensor_tensor(
            out=dst, in0=qb[qs], scalar=g8[:, g - 2:g - 1], in1=rr[qs], op0=ALU.mult, op1=ALU.mult),
            reads=[buf("qb%d" % qs), buf("rr%d" % qs), buf("g8")], writes=[buf("QT%d" % g)])

    def stB1(bi):
        ns = bi % 2
        for kc in range(8):
            P.op("pe", lambda e, kc=kc, ns=ns: e.transpose(tps[ns][:, kc * 128:(kc + 1) * 128],
                                                           xn[ns][:, kc * 128:(kc + 1) * 128], ident),
                 reads=[buf("xn%d" % ns), buf("cb")], writes=[bank[ns]])

    def stB2(bi):
        ns, hs, sub = bi % 2, (bi // 4) % 2, bi % 4
        P.op("dve", lambda e, ns=ns, hs=hs, sub=sub: e.tensor_copy(
            out=hT[hs][:, :, sub * 128:(sub + 1) * 128], in_=tps[ns].rearrange("p (k t) -> p k t", k=8)),
            reads=[bank[ns]], writes=[buf("hT%d" % hs)])

    dq_mm, dq_ev = [], []
    for k in range(NBLK + 12):
        if k < NBLK:
            stA(k)
        if 1 <= k <= NBLK:
            stA2(k - 1)
        if 4 <= k <= NBLK + 3:
            stC1(k - 4)
        if 1 <= k <= NBLK:
            stA3(k - 1)
        if 2 <= k <= NBLK + 1:
            stB1(k - 2)
        if 3 <= k <= NBLK + 2:
            stB2(k - 3)
            if (k - 3) % 4 == 3:
                ti = (k - 3) // 4
                for g in range(4):
                    dq_mm.append((ti, g))
        if 5 <= k <= NBLK + 4:
            stC2(k - 5)
        nev = len(dq_ev)
        for _ in range(nev):
            kind, ti, g = dq_ev.pop(0)
            {"ev": stD_ev, "fin": stD_fin}[kind](ti, g)
            if kind == "ev" and g >= 2:
                dq_ev.append(("fin", ti, g))
        for _ in range(2):
            if dq_mm:
                ti, g = dq_mm.pop(0)
                stD_mm(ti, g)
                dq_ev.append(("ev", ti, g))
    while dq_mm or dq_ev:
        nev = len(dq_ev)
        for _ in range(nev):
            kind, ti, g = dq_ev.pop(0)
            {"ev": stD_ev, "fin": stD_fin}[kind](ti, g)
            if kind == "ev" and g >= 2:
                dq_ev.append(("fin", ti, g))
        if dq_mm:
            ti, g = dq_mm.pop(0)
            stD_mm(ti, g)
            dq_ev.append(("ev", ti, g))

    if phase == 2:
        if debug:
            for nm, t, w in (("QsbT", QsbT, S), ("KsbT", KsbT, S), ("QdfT", QdfT, S), ("KdfT", KdfT, S),
                             ("Vsb", Vsb, NB * 128), ("Vdf", Vdf, NB * 130)):
                d = nc.dram_tensor("dbg_" + nm, [128, w], BF16, kind="ExternalOutput").ap()
                dbg[nm] = d
                P.barrier()
                P.dma("sp", lambda e, d=d, t=t: e.dma_start(out=d, in_=t))
            d = nc.dram_tensor("dbg_small", [128, 1024], F32, kind="ExternalOutput").ap()
            P.dma("sp", lambda e, d=d: e.dma_start(out=d, in_=small))
            d2 = nc.dram_tensor("dbg_gsb", [S, 128], F32, kind="ExternalOutput").ap()
            P.dma("sp", lambda e, d2=d2: e.dma_start(out=d2, in_=gsb_d))
            d3 = nc.dram_tensor("dbg_gdf", [S, 128], F32, kind="ExternalOutput").ap()
            P.dma("sp", lambda e, d3=d3: e.dma_start(out=d3, in_=gdf_d))
        return _finish(nc, P, B, dbg, locals())

    P.barrier()
    e_t = [scr[:, i * 1024:(i + 1) * 1024].rearrange("p (h t) -> p h t", h=2) for i in range(2)]
    spb = [scr[:, 2048 + i * 512:2048 + (i + 1) * 512].bitcast(BF16).rearrange("p (h t) -> p h t", h=2) for i in range(2)]
    Lsum = scr[:, 3072:4096].rearrange("p (h t) -> p h t", h=2)
    LsB = [scr[:, 4096 + i * 512:4096 + (i + 1) * 512].bitcast(BF16).rearrange("p (h t) -> p h t", h=2) for i in range(2)]
    ATt = [scr[:, 5120 + i * 512:5120 + (i + 1) * 512].bitcast(BF16).rearrange("p (h t) -> p h t", h=2) for i in range(3)]
    gt = [scr[:, 6656 + i * 512:6656 + (i + 1) * 512].rearrange("p (s f) -> p s f", s=4) for i in range(2)]
    mix = [scr[:, 7680 + i * 256:7680 + (i + 1) * 256].bitcast(BF16).rearrange("p (s f) -> p s f", s=4) for i in range(2)]
    mixT = [scr[:, 8192 + i * 256:8192 + (i + 1) * 256].bitcast(BF16) for i in range(2)]
    tpb = ps[:, 6, :].bitcast(BF16)

    def sb_steps():
        out = []
        for qi in range(DBG.get('sb_tiles', NT)):
            for m in range(4 * qi + 4):
                kb = 4 * qi + 3 - m
                j = kb - 4 * qi
                out.append(dict(qi=qi, m=m, kb=kb, j=j, c0=128 * max(j, 0), last=(kb == 0)))
        for gi, st in enumerate(out):
            st["gi"] = gi
        return out

    steps = sb_steps()
    nst = len(steps)

    def QK(st):
        sl = st["gi"] % 2
        c0, kb, qi = st["c0"], st["kb"], st["qi"]
        for h in range(2):
            P.op("pe", lambda e, h=h, sl=sl, c0=c0, kb=kb, qi=qi: e.matmul(
                ps[:, 2 * sl + h, c0:512], lhsT=KsbT[64 * h:64 * h + 64, kb * 128:(kb + 1) * 128],
                rhs=QsbT[64 * h:64 * h + 64, qi * 512 + c0:(qi + 1) * 512], start=True, stop=False, skip_group_check=True),
                reads=[buf("QT0"), buf("QT1")], writes=[bank[2 * sl + h]])
        if st["j"] >= 0:
            for h in range(2):
                P.op("pe", lambda e, h=h, sl=sl, c0=c0: e.matmul(
                    ps[:, 2 * sl + h, c0:c0 + 128], lhsT=ident, rhs=negsb, start=False, stop=False, skip_group_check=True),
                    reads=[buf("cb")], writes=[bank[2 * sl + h]])

    def EXP1(st):
        sl = st["gi"] % 2
        c0 = st["c0"]
        P.op("act", lambda e, sl=sl, c0=c0: e.activation(out=e_t[sl][:, :, c0:512], in_=ps[:, 2 * sl:2 * sl + 2, c0:512],
                                                         func=AF.Exp, scale=0.125),
             reads=[bank[2 * sl], bank[2 * sl + 1]], writes=[buf("e%d" % sl)])

    def LN(st):
        sl = st["gi"] % 2
        c0 = st["c0"]
        P.op("act", lambda e, sl=sl, c0=c0: e.activation(out=spb[sl][:, :, c0:512], in_=e_t[sl][:, :, c0:512],
                                                         func=AF.Ln, bias=1.0),
             reads=[buf("e%d" % sl)], writes=[buf("sp%d" % sl)])

    def LSUM(st):
        gi, sl, c0 = st["gi"], st["gi"] % 2, st["c0"]
        if st["m"] == 0:
            P.op("pool", lambda e: e.memset(Lsum, 0.0), writes=[buf("Lsum")])
        P.op("dve", lambda e, sl=sl, c0=c0: e.tensor_tensor(out=Lsum[:, :, c0:512], in0=Lsum[:, :, c0:512],
                                                            in1=spb[sl][:, :, c0:512], op=ALU.add),
             reads=[buf("sp%d" % sl)], writes=[buf("Lsum")])
        if not st["last"]:
            nxt = steps[gi + 1]
            lb = (gi + 1) % 2
            cn = nxt["c0"]
            P.op("dve", lambda e, lb=lb, cn=cn: e.tensor_copy(out=LsB[lb][:, :, cn:512], in_=Lsum[:, :, cn:512]),
                 reads=[buf("Lsum")], writes=[buf("LsB%d" % lb)])

    def TRI(st):
        gi, sl, c0 = st["gi"], st["gi"] % 2, st["c0"]
        for h in range(2):
            P.op("pe", lambda e, h=h, sl=sl, c0=c0: e.matmul(
                ps[:, 2 * sl + h, c0:512], lhsT=tri, rhs=spb[sl][:, h, c0:512], start=False, stop=True,
                skip_group_check=True),
                reads=[buf("sp%d" % sl), buf("cb")], writes=[bank[2 * sl + h]])

    def CARRY(st):
        gi, sl, c0 = st["gi"], st["gi"] % 2, st["c0"]
        if st["m"] == 0:
            return
        lb = gi % 2
        for h in range(2):
            P.op("pe", lambda e, h=h, sl=sl, c0=c0, lb=lb: e.matmul(
                ps[:, 2 * sl + h, c0:512], lhsT=onesm8, rhs=LsB[lb][:, h, c0:512], start=False, stop=False,
                skip_group_check=True),
                reads=[buf("LsB%d" % lb), buf("cb")], writes=[bank[2 * sl + h]])

    def EXP2(st):
        gi, sl, c0 = st["gi"], st["gi"] % 2, st["c0"]
        ai = gi % 3
        P.op("act", lambda e, sl=sl, c0=c0, ai=ai: e.activation(out=ATt[ai][:, :, c0:512], in_=ps[:, 2 * sl:2 * sl + 2, c0:512],
                                                                func=AF.Exp, scale=0.125),
             reads=[bank[2 * sl], bank[2 * sl + 1]], writes=[buf("AT%d" % ai)])

    def AV(st):
        gi, qi, kb = st["gi"], st["qi"], st["kb"]
        ai = gi % 3
        ob = 4 + qi % 2
        first = st["m"] == 0
        for sbk in range(max(st["j"], 0), 4):
            for h in range(2):
                P.op("pe", lambda e, sbk=sbk, h=h, ai=ai, ob=ob, kb=kb, first=first, last=st["last"]: e.matmul(
                    ps[:, ob, sbk * 128 + h * 64:sbk * 128 + h * 64 + 64], lhsT=ATt[ai][:, h, sbk * 128:(sbk + 1) * 128],
                    rhs=Vsb[:, kb * 128 + h * 64:kb * 128 + h * 64 + 64], start=first, stop=last, skip_group_check=True),
                    reads=[buf("AT%d" % ai), buf("Vsb")], writes=[bank[ob]])
                first = False

    def EPI_LOAD(qi):
        gs_ = qi % 2
        P.dma("sp", lambda e, qi=qi, gs_=gs_: e.dma_start(
            out=gt[gs_], in_=gsb_d[qi * 512:(qi + 1) * 512, :].rearrange("(s p) f -> p s f", p=128)),
            writes=[buf("gt%d" % gs_)])

    def EPI_A(qi, row0, gsrc):
        gs_ = qi % 2
        ob = 4 + qi % 2
        P.op("dve", lambda e, gs_=gs_, ob=ob: e.tensor_tensor(out=mix[gs_], in0=ps[:, ob, :].rearrange("p (s f) -> p s f", s=4),
                                                              in1=gt[gs_], op=ALU.mult),
             reads=[bank[ob], buf("gt%d" % gs_)], writes=[buf("mix%d" % gs_)])

    def EPI_B(qi, row0):
        gs_ = qi % 2
        for sbk in range(4):
            P.op("pe", lambda e, sbk=sbk, gs_=gs_: e.transpose(tpb[:, sbk * 128:(sbk + 1) * 128], mix[gs_][:, sbk, :], ident),
                 reads=[buf("mix%d" % gs_), buf("cb")], writes=[bank[6]])
        P.op("dve", lambda e, gs_=gs_: e.tensor_copy(out=mixT[gs_], in_=tpb[:, 0:512]),
             reads=[bank[6]], writes=[buf("mixT%d" % gs_)])
        c_ = ch_of[qi]
        o_ = (qi - CH[c_][0]) * 512
        t = P.dma("sp", lambda e, gs_=gs_, c_=c_, o_=o_, row0=row0: e.dma_start(
            out=mixT_q[c_][row0:row0 + 128, o_:o_ + 512], in_=mixT[gs_]),
            reads=[buf("mixT%d" % gs_)])
        mix_tickets[c_].append(t)

    p5_queue = []
    p5_hook = {}

    def run_sb():
        pending = []
        QK(steps[0])
        if nst > 1:
            QK(steps[1])
        EXP1(steps[0])
        LN(steps[0])
        LSUM(steps[0])
        EPI_LOAD(0)
        for i in range(nst):
            st = steps[i]
            TRI(st)
            if i >= 1:
                pv = steps[i - 1]
                AV(pv)
                if pv["last"]:
                    EPI_A(pv["qi"], 0, gsb_d)
                    pending.append(pv["qi"])
                    if pv["qi"] + 1 < DBG.get('sb_tiles', NT):
                        EPI_LOAD(pv["qi"] + 1)
            if p5_queue and p5_queue[0][0] <= i and i % DBG.get('p5_every', 7) == 0 and not pending:
                p5_hook["blk"](p5_queue.pop(0)[1])
            if i + 1 < nst:
                EXP1(steps[i + 1])
            EXP2(st)
            if i + 1 < nst:
                LN(steps[i + 1])
                LSUM(steps[i + 1])
            if pending and not (i >= 1 and steps[i - 1]["last"]):
                qd = pending.pop(0)
                EPI_B(qd, 0)
                if qd in ch_end:
                    AGQ(ch_end[qd], i)
            if i + 2 < nst:
                QK(steps[i + 2])
            if i + 1 < nst:
                CARRY(steps[i + 1])
        AV(steps[nst - 1])
        EPI_A(steps[nst - 1]["qi"], 0, gsb_d)
        pending.append(steps[nst - 1]["qi"])
        while pending:
            qd = pending.pop(0)
            EPI_B(qd, 0)
            if qd in ch_end:
                AGQ(ch_end[qd], nst)

    def AGQ(q, step_i):
        if phase <= 4:
            return
        P.coll(lambda e, q=q: e.collective_compute("AllGather", ALU.bypass, replica_groups=[[0, 1, 2, 3], [4, 5, 6, 7]],
                                                   ins=[mixT_q[q].opt()], outs=[gath_q[q].opt()]),
               writes=[buf("gath%d" % q)], extra=list(mix_tickets[q]))
        for b_ in range(CH[q][0] * 4, CH[q][1] * 4):
            p5_queue.append((step_i + DBG.get('p5_delay', 60), b_))

    ETt = [scr[:, 8704 + i * 512:8704 + (i + 1) * 512].bitcast(BF16).rearrange("p (h t) -> p h t", h=2) for i in range(3)]
    gtd = [scr[:, 10240 + i * 512:10240 + (i + 1) * 512].rearrange("p (s f) -> p s f", s=4) for i in range(2)]
    t1 = [scr[:, 11264 + i * 128:11264 + (i + 1) * 128] for i in range(2)]
    dd = [scr[:, 11520 + i * 512:11520 + (i + 1) * 512].rearrange("p (s f) -> p s f", s=4) for i in range(2)]
    mixd = [scr[:, 12544 + i * 256:12544 + (i + 1) * 256].bitcast(BF16).rearrange("p (s f) -> p s f", s=4) for i in range(2)]
    mixTd = [scr[:, 13056 + i * 256:13056 + (i + 1) * 256].bitcast(BF16) for i in range(2)]
    stat = [scr[:, 13568 + i * 32:13568 + (i + 1) * 32] for i in range(2)]
    junkd = scr[:, 13696:13824]
    tpd = ps[:, 7, :].bitcast(BF16)

    def oreg(mp, sbk):
        r = mp * 4 + sbk
        return 4 + r // 3, (r % 3) * 132

    def df_steps():
        out = []
        for qi in range(DBG.get('df_tiles', NT)):
            for m in range(4 * qi + 4):
                kb = 4 * qi + 3 - m
                j = kb - 4 * qi
                out.append(dict(qi=qi, m=m, kb=kb, j=j, c0=128 * max(j, 0), last=(kb == 0)))
        for gi, st in enumerate(out):
            st["gi"] = gi
        return out

    dsteps = df_steps()
    nds = len(dsteps)

    def QKD(st):
        sl = st["gi"] % 2
        c0, kb, qi = st["c0"], st["kb"], st["qi"]
        diag = st["j"] >= 0
        for mp in range(2):
            P.op("pe", lambda e, mp=mp, sl=sl, c0=c0, kb=kb, qi=qi, diag=diag: e.matmul(
                ps[:, 2 * sl + mp, c0:512], lhsT=KdfT[64 * mp:64 * mp + 64, kb * 128:(kb + 1) * 128],
                rhs=QdfT[64 * mp:64 * mp + 64, qi * 512 + c0:(qi + 1) * 512], start=True, stop=(not diag), skip_group_check=True),
                reads=[buf("QT2"), buf("QT3")], writes=[bank[2 * sl + mp]])
        if diag:
            for mp in range(2):
                P.op("pe", lambda e, mp=mp, sl=sl, c0=c0: e.matmul(
                    ps[:, 2 * sl + mp, c0:c0 + 128], lhsT=ident, rhs=negdf, start=False, stop=True, skip_group_check=True),
                    reads=[buf("cb")], writes=[bank[2 * sl + mp]])

    def EXPD(st):
        gi, sl, c0, m = st["gi"], st["gi"] % 2, st["c0"], st["m"]
        ai = gi % 3
        P.op("act", lambda e, c0=c0, m=m, sl=sl, ai=ai: e.activation(
            out=ETt[ai][:, :, c0:512], in_=ps[:, 2 * sl:2 * sl + 2, c0:512], func=AF.Exp, scale=0.125,
            bias=cf[:, CF_ALI + m:CF_ALI + m + 1]),
            reads=[bank[2 * sl], bank[2 * sl + 1], buf("cf")], writes=[buf("ET%d" % ai)])

    started = set()

    def AVD(st):
        gi, qi, kb = st["gi"], st["qi"], st["kb"]
        ai = gi % 3
        if st["m"] == 0:
            started.clear()
        for mp in range(2):
            for sbk in range(max(st["j"], 0), 4):
                bk, off = oreg(mp, sbk)
                first = bk not in started
                started.add(bk)
                P.op("pe", lambda e, mp=mp, sbk=sbk, ai=ai, bk=bk, off=off, kb=kb, first=first, last=st["last"]: e.matmul(
                    ps[:, bk, off:off + 129], lhsT=ETt[ai][:, mp, sbk * 128:(sbk + 1) * 128],
                    rhs=Vdf3[:, kb, 0:129], start=first, stop=last, skip_group_check=True),
                    reads=[buf("ET%d" % ai), buf("Vdf")], writes=[bank[bk]])

    def EPID_LOAD(qi):
        gs_ = qi % 2
        P.dma("sp", lambda e, qi=qi, gs_=gs_: e.dma_start(
            out=gtd[gs_], in_=gdf_d[qi * 512:(qi + 1) * 512, :].rearrange("(s p) f -> p s f", p=128)),
            writes=[buf("gtd%d" % gs_)])

    Ocp = scr[:, 0:1188].rearrange("p (b c) -> p b c", b=3)
    ocp_bufs = [buf("e0"), buf("e1")]

    def ocp(mp, sbk):
        r = mp * 4 + sbk
        return r // 3, (r % 3) * 132

    def EPID_A0(qi):
        for b_ in range(3):
            P.op("dve", lambda e, b_=b_: e.tensor_copy(out=Ocp[:, b_, :], in_=ps[:, 4 + b_, 0:396]),
                 reads=[bank[4 + b_]], writes=ocp_bufs)

    def EPID_A1(qi):
        gs_ = qi % 2
        stt = stat[gs_]
        for sbk in range(4):
            b1, o1 = ocp(0, sbk)
            b2, o2 = ocp(1, sbk)
            ts = sbk % 2
            P.op("dve", lambda e, b1=b1, o1=o1, sbk=sbk, stt=stt: e.reciprocal(out=stt[:, sbk:sbk + 1], in_=Ocp[:, b1, o1 + 128:o1 + 129]),
                 reads=ocp_bufs, writes=[buf("stat%d" % gs_)])
            P.op("dve", lambda e, b2=b2, o2=o2, sbk=sbk, stt=stt: e.reciprocal(out=stt[:, 4 + sbk:5 + sbk], in_=Ocp[:, b2, o2 + 128:o2 + 129]),
                 reads=ocp_bufs, writes=[buf("stat%d" % gs_)])
            P.op("dve", lambda e, sbk=sbk, stt=stt: e.tensor_tensor(out=stt[:, 8 + sbk:9 + sbk], in0=stt[:, 4 + sbk:5 + sbk], in1=nlam, op=ALU.mult),
                 reads=[buf("stat%d" % gs_), buf("nlam")], writes=[buf("statb%d" % gs_)])
            P.op("dve", lambda e, b1=b1, o1=o1, sbk=sbk, stt=stt, ts=ts: e.tensor_scalar(
                out=t1[ts], in0=Ocp[:, b1, o1:o1 + 128], scalar1=stt[:, sbk:sbk + 1], scalar2=None, op0=ALU.mult),
                reads=ocp_bufs + [buf("stat%d" % gs_)], writes=[buf("t1_%d" % ts)])
            P.op("dve", lambda e, b2=b2, o2=o2, sbk=sbk, stt=stt, ts=ts, gs_=gs_: e.scalar_tensor_tensor(
                out=dd[gs_][:, sbk, :], in0=Ocp[:, b2, o2:o2 + 128], scalar=stt[:, 8 + sbk:9 + sbk], in1=t1[ts],
                op0=ALU.mult, op1=ALU.add),
                reads=ocp_bufs + [buf("statb%d" % gs_), buf("t1_%d" % ts)], writes=[buf("dd%d" % gs_)])

    sqd = scr[:, 0:512].rearrange("p (s f) -> p s f", s=4)

    def EPID_A2(qi):
        gs_ = qi % 2
        stt = stat[gs_]
        P.op("dve", lambda e, gs_=gs_: e.tensor_tensor(out=sqd, in0=dd[gs_], in1=dd[gs_], op=ALU.mult),
             reads=[buf("dd%d" % gs_)], writes=ocp_bufs)
        P.op("dve", lambda e, stt=stt: e.reduce_sum(out=stt[:, 12:16], in_=sqd, axis=AX.X),
             reads=ocp_bufs, writes=[buf("ssd%d" % gs_)])
        P.op("act", lambda e, stt=stt: e.activation(out=stt[:, 16:20], in_=stt[:, 12:16], func=AF.Ln, scale=1.0 / 128.0, bias=float(EPS)),
             reads=[buf("ssd%d" % gs_)], writes=[buf("rstd%d" % gs_)])
        P.op("act", lambda e, stt=stt: e.activation(out=stt[:, 16:20], in_=stt[:, 16:20], func=AF.Exp, scale=-0.5),
             reads=[buf("rstd%d" % gs_)], writes=[buf("rstd%d" % gs_)])

    def EPID_A3(qi):
        gs_ = qi % 2
        stt = stat[gs_]
        for sbk in range(4):
            P.op("dve", lambda e, sbk=sbk, stt=stt, gs_=gs_: e.scalar_tensor_tensor(
                out=mixd[gs_][:, sbk, :], in0=dd[gs_][:, sbk, :], scalar=stt[:, 16 + sbk:17 + sbk], in1=gtd[gs_][:, sbk, :],
                op0=ALU.mult, op1=ALU.mult),
                reads=[buf("dd%d" % gs_), buf("rstd%d" % gs_), buf("gtd%d" % gs_)], writes=[buf("mixd%d" % gs_)])

    def EPID_B(qi):
        gs_ = qi % 2
        for sbk in range(4):
            P.op("pe", lambda e, sbk=sbk, gs_=gs_: e.transpose(tpd[:, sbk * 128:(sbk + 1) * 128], mixd[gs_][:, sbk, :], ident),
                 reads=[buf("mixd%d" % gs_), buf("cb")], writes=[bank[7]])
        P.op("dve", lambda e, gs_=gs_: e.tensor_copy(out=mixTd[gs_], in_=tpd[:, 0:512]),
             reads=[bank[7]], writes=[buf("mixTd%d" % gs_)])
        c_ = ch_of[qi]
        o_ = (qi - CH[c_][0]) * 512
        t = P.dma("sp", lambda e, gs_=gs_, c_=c_, o_=o_: e.dma_start(
            out=mixT_q[c_][128:256, o_:o_ + 512], in_=mixTd[gs_]),
            reads=[buf("mixTd%d" % gs_)])
        mix_tickets[c_].append(t)

    def run_df():
        pend = []
        stages = [EPID_A1, EPID_A2, EPID_A3, EPID_B]
        delays = [0, 7, 2, 1]

        def advance(force=False):
            for it in list(pend):
                if it[0] > 0 and not force:
                    it[0] -= 1
                    continue
                stages[it[1]](it[2])
                it[1] += 1
                if it[1] == len(stages):
                    pend.remove(it)
                else:
                    it[0] = min(delays[it[1]], 2) if it[2] == 0 else delays[it[1]]

        EPID_LOAD(0)
        QKD(dsteps[0])
        for i in range(nds):
            st = dsteps[i]
            if i + 1 < nds:
                QKD(dsteps[i + 1])
            EXPD(st)
            if i >= 1:
                pv = dsteps[i - 1]
                AVD(pv)
                if pv["last"]:
                    EPID_A0(pv["qi"])
                    pend.append([0, 0, pv["qi"]])
                    if pv["qi"] + 1 < DBG.get('df_tiles', NT):
                        EPID_LOAD(pv["qi"] + 1)
            advance()
        AVD(dsteps[nds - 1])
        EPID_A0(dsteps[nds - 1]["qi"])
        pend.append([0, 0, dsteps[nds - 1]["qi"]])
        while pend:
            advance(force=True)

    mall = [scr[:, 13824 + i * 1024:13824 + (i + 1) * 1024].bitcast(BF16).rearrange("p (k t) -> p k t", k=8) for i in range(2)]
    woutb = scr[:, 15872:16896].bitcast(BF16).rearrange("p (k n) -> p k n", k=8)
    wos = [scr[:, 16896 + i * 256:16896 + (i + 1) * 256] for i in range(2)]
    xqt = [scr[:, 17408 + i * 256:17408 + (i + 1) * 256] for i in range(2)]
    ot = [scr[:, 17920 + i * 256:17920 + (i + 1) * 256] for i in range(2)]
    gath_v = [g.rearrange("(k p) t -> p k t", p=128) for g in gath_q]
    p5ps = ps[:, 7, 256:512]

    def p5_prep():
        for kc in range(8):
            sl = kc % 2
            P.dma("sp", lambda e, kc=kc, sl=sl: e.dma_start(out=wos[sl], in_=wout_d[kc * 128:(kc + 1) * 128, :]), writes=[buf("wos%d" % sl)])
            P.op("dve", lambda e, kc=kc, sl=sl: e.tensor_copy(out=woutb[:, kc, :], in_=wos[sl]), reads=[buf("wos%d" % sl)], writes=[buf("woutb")])

    p5_loaded = set()

    def p5_load(blk):
        if blk in p5_loaded:
            return
        p5_loaded.add(blk)
        q = ch_of[blk // 4]
        lb_ = blk - CH[q][0] * 4
        ms = (blk // 2) % 2
        xs = blk % 2
        if blk % 2 == 0:
            P.dma("sp", lambda e, q=q, lb_=lb_, ms=ms: e.dma_start(out=mall[ms], in_=gath_v[q][:, :, lb_ * 128:lb_ * 128 + 256]),
                  reads=[buf("gath%d" % q)], writes=[buf("mall%d" % ms)])
        P.dma("sp", lambda e, blk=blk, xs=xs: e.dma_start(out=xqt[xs], in_=xq_d[blk * 128:(blk + 1) * 128, :]),
              writes=[buf("xqt%d" % xs)])

    def p5_block(blk):
        ms = (blk // 2) % 2
        xs = blk % 2
        os_ = blk % 2
        p5_load(blk)
        tb = blk % 2
        for kc in range(8):
            P.op("pe", lambda e, kc=kc, ms=ms, tb=tb: e.matmul(
                p5ps, lhsT=mall[ms][:, kc, tb * 128:(tb + 1) * 128], rhs=woutb[:, kc, :],
                start=(kc == 0), stop=(kc == 7)),
                reads=[buf("mall%d" % ms), buf("woutb")], writes=[bank[7]])
        P.op("dve", lambda e, os_=os_: e.tensor_tensor(out=ot[os_], in0=p5ps, in1=gate_rep, op=ALU.mult),
             reads=[bank[7], buf("gate")], writes=[buf("ot%d" % os_)])
        P.op("pool", lambda e, os_=os_, xs=xs: e.tensor_tensor(out=ot[os_], in0=ot[os_], in1=xqt[xs], op=ALU.add),
             reads=[buf("xqt%d" % xs)], writes=[buf("ot%d" % os_)])
        P.dma("pool", lambda e, blk=blk, os_=os_: e.dma_start(out=out_d[blk * 128:(blk + 1) * 128, :], in_=ot[os_]),
              reads=[buf("ot%d" % os_)])
        for (_, nb_) in p5_queue[:2]:
            if nb_ // 2 <= blk // 2 + 1:
                p5_load(nb_)

    p5_hook["blk"] = p5_block

    run_df()
    if phase == 4:
        return _finish(nc, P, B, dbg, locals())
    p5_prep()
    run_sb()
    while p5_queue:
        p5_block(p5_queue.pop(0)[1])
    return _finish(nc, P, B, dbg, locals())


def _finish(nc, P, B, dbg, L):
    P.barrier()
    P.emit(nc)
    return nc


def _col_index(hg):
    r = np.arange(128)
    sbo = 128 * hg + r
    return np.concatenate([sbo, 512 + sbo, 2048 + sbo, 2560 + sbo,
                           1024 + sbo, 1536 + sbo, 3072 + sbo, 3584 + sbo])


def make_in_maps(x, c, norm_g, w_ada, b_ada, w_in, q_norm_g, k_norm_g,
                 lambda_q1, lambda_k1, lambda_q2, lambda_k2, subln_g, w_out):
    f = np.float32
    x = np.asarray(x, f); c = np.asarray(c, f)
    w_ada = np.asarray(w_ada, f)[0]; b_ada = np.asarray(b_ada, f)[0]
    w_in = np.asarray(w_in, f)[0]; w_out = np.asarray(w_out, f)[0]
    norm_g = np.asarray(norm_g, f)[0]
    qg = np.asarray(q_norm_g, f)[0]; kg = np.asarray(k_norm_g, f)[0]
    lamv = np.concatenate([np.asarray(a, f)[0] for a in (lambda_q1, lambda_k1, lambda_q2, lambda_k2)])
    subg = np.asarray(subln_g, f)[0]
    wss = np.ascontiguousarray(w_ada[:, 0:2048])
    bcol = np.ascontiguousarray(b_ada[0:2048].reshape(16, 128).T)
    ngcol = np.ascontiguousarray(norm_g.reshape(8, 128).T)
    qkg = np.ascontiguousarray(np.stack([np.concatenate([qg, qg]), np.concatenate([kg, kg])], axis=1))
    rows = np.concatenate([np.concatenate([128 * r + np.arange(128), 512 + 128 * r + np.arange(128)]) for r in range(4)])
    w_out_p = w_out[rows]
    maps = []
    for core in range(8):
        b, hg = core // 4, core % 4
        cb, cf = _consts(hg)
        maps.append({
            "x": np.ascontiguousarray(x[b]),
            "xq": np.ascontiguousarray(x[b][:, 256 * hg:256 * (hg + 1)]),
            "ccol": np.ascontiguousarray(c[b].reshape(8, 128).T),
            "ngcol": ngcol, "bcol": bcol, "wss": wss,
            "wg": np.ascontiguousarray(w_ada[:, 2048 + 256 * hg:2048 + 256 * (hg + 1)]),
            "bg": np.ascontiguousarray(b_ada[2048 + 256 * hg:2048 + 256 * (hg + 1)]),
            "win": np.ascontiguousarray(w_in[:, _col_index(hg)]),
            "qkg": qkg, "lamv": lamv, "subg": subg,
            "wout": np.ascontiguousarray(w_out_p[:, 256 * hg:256 * (hg + 1)]),
            "cb": cb, "cf": cf,
        })
    return maps


_NC_CACHE = {}


def kernel(**inputs):
    in_maps = make_in_maps(**inputs)
    if "nc" not in _NC_CACHE:
        _NC_CACHE["nc"] = build()
    res = run_bass_kernel_spmd(_NC_CACHE["nc"], in_maps, core_ids=list(range(8)))
    out = np.empty((2, S, D), np.float32)
    for core in range(8):
        b, hg = core // 4, core % 4
        out[b][:, 256 * hg:256 * (hg + 1)] = res.results[core]["out"]
    return out
```
